# Optimizing a Trainium2 kernel written in Bass

```python
import jax, jax.numpy as jnp
from jax import lax
import numpy as np

D_MODEL = 1024
BATCH = 8
SEQ = 8192
DEPTH = 2
DEC_BATCH = 4
DEC_SEQ = 4096
PAST_LEN = 128

HEAD_DIM = 64
N_HEADS_A = 4
DILATED_PATTERNS = ((128, 1), (512, 4), (2048, 16))
N_HEADS_B = 8
N_KV_B = 2
GQA_GROUP = N_HEADS_B // N_KV_B
N_GROUPS_C = 4
D_A = N_HEADS_A * HEAD_DIM
D_B = N_HEADS_B * HEAD_DIM
D_KV_B = N_KV_B * HEAD_DIM
D_C = N_GROUPS_C * HEAD_DIM
D_MIX = D_A + D_B + D_C
N_MIX_HEADS = D_MIX // HEAD_DIM
D_IN = 3 * D_A + D_B + 2 * D_KV_B + D_C
IN_SPLITS = (D_A, 2 * D_A, 3 * D_A, 3 * D_A + D_B, 3 * D_A + D_B + D_KV_B, 3 * D_A + D_B + 2 * D_KV_B)
D_FF = 4 * D_MODEL
CONV_W = 3
ROPE_THETA_PARTIAL = 500000.0
ROT_DIM_PARTIAL = HEAD_DIM // 4
ROPE_THETA_AXIAL = 10000.0
GRID_W = 64
Q_BLOCK = 128
RMS_EPS = 1e-6
NEG_INF = -1e30

kernel_name = "hybrid_dilated_grid_fourier_encoder"


def _rms_norm(x, g):
    xf = x.astype(jnp.float32)
    y = xf * lax.rsqrt(jnp.mean(xf * xf, axis=-1, keepdims=True) + RMS_EPS)
    return (y * g.astype(jnp.float32)).astype(x.dtype)


def _rope_tables(pos, rot_dim, theta):
    half = rot_dim // 2
    inv = jnp.power(jnp.float32(theta), -jnp.arange(half, dtype=jnp.float32) / half)
    ang = pos.astype(jnp.float32)[:, None] * inv[None, :]
    return jnp.cos(ang), jnp.sin(ang)


def _apply_rope(x, cos, sin):
    half = x.shape[-1] // 2
    x1 = x[..., :half].astype(jnp.float32)
    x2 = x[..., half:].astype(jnp.float32)
    c = cos[:, None, :]
    s = sin[:, None, :]
    return jnp.concatenate([x1 * c - x2 * s, x2 * c + x1 * s], axis=-1).astype(x.dtype)


def _partial_rope(x, cos, sin):
    return jnp.concatenate([_apply_rope(x[..., :ROT_DIM_PARTIAL], cos, sin), x[..., ROT_DIM_PARTIAL:]], axis=-1)


def _axial_rope(x, cos_r, sin_r, cos_c, sin_c):
    half = HEAD_DIM // 2
    return jnp.concatenate([_apply_rope(x[..., :half], cos_r, sin_r),
                            _apply_rope(x[..., half:], cos_c, sin_c)], axis=-1)


def _position_tables(seq_len):
    rows = seq_len // GRID_W
    t = jnp.arange(seq_len, dtype=jnp.int32)
    row = jnp.broadcast_to(jnp.arange(rows, dtype=jnp.int32)[:, None], (rows, GRID_W)).reshape(-1)
    col = jnp.broadcast_to(jnp.arange(GRID_W, dtype=jnp.int32)[None, :], (rows, GRID_W)).reshape(-1)
    cos_t, sin_t = _rope_tables(t, ROT_DIM_PARTIAL, ROPE_THETA_PARTIAL)
    cos_r, sin_r = _rope_tables(row, HEAD_DIM // 2, ROPE_THETA_AXIAL)
    cos_c, sin_c = _rope_tables(col, HEAD_DIM // 2, ROPE_THETA_AXIAL)
    return (cos_t, sin_t, cos_r, sin_r, cos_c, sin_c)


def _dilated_attention(q, k, v):
    B, S, H, Dh = q.shape
    scale = Dh ** -0.5
    n_blocks = S // Q_BLOCK

    def block(i):
        q0 = i * Q_BLOCK
        qb = lax.dynamic_slice_in_dim(q, q0, Q_BLOCK, axis=1)
        qpos = q0 + jnp.arange(Q_BLOCK, dtype=jnp.int32)
        scores, idxs, n_keys = [], [], []
        for window, dil in DILATED_PATTERNS:
            side = window // (2 * dil)
            offs = jnp.arange(-side, side + 1, dtype=jnp.int32) * dil
            pos = qpos[:, None] + offs[None, :]
            valid = (pos >= 0) & (pos < S)
            idx = jnp.clip(pos, 0, S - 1)
            kg = jnp.take(k, idx.reshape(-1), axis=1).reshape(B, Q_BLOCK, offs.shape[0], H, Dh)
            s = jnp.einsum("bqhd,bqkhd->bhqk", qb, kg, preferred_element_type=jnp.float32) * scale
            scores.append(jnp.where(valid[None, None], s, NEG_INF))
            idxs.append(idx)
            n_keys.append(offs.shape[0])
        p = jax.nn.softmax(jnp.concatenate(scores, axis=-1), axis=-1)
        out = jnp.zeros((B, Q_BLOCK, H, Dh), jnp.float32)
        start = 0
        for idx, nk in zip(idxs, n_keys):
            vg = jnp.take(v, idx.reshape(-1), axis=1).reshape(B, Q_BLOCK, nk, H, Dh)
            out = out + jnp.einsum("bhqk,bqkhd->bqhd", p[..., start:start + nk],
                                   vg.astype(jnp.float32))
            start += nk
        return out.astype(v.dtype)

    o = lax.map(block, jnp.arange(n_blocks, dtype=jnp.int32))
    return o.transpose(1, 0, 2, 3, 4).reshape(B, S, H, Dh)


def _grid_attention(q, k, v):
    B, S, HQ, Dh = q.shape
    scale = Dh ** -0.5
    qg = q.reshape(B, S, N_KV_B, GQA_GROUP, Dh)

    def block(i):
        qb = lax.dynamic_slice_in_dim(qg, i * Q_BLOCK, Q_BLOCK, axis=1)
        s = jnp.einsum("bqgrd,bkgd->bgrqk", qb, k, preferred_element_type=jnp.float32) * scale
        p = jax.nn.softmax(s, axis=-1)
        return jnp.einsum("bgrqk,bkgd->bqgrd", p.astype(v.dtype), v)

    o = lax.map(block, jnp.arange(S // Q_BLOCK, dtype=jnp.int32))
    return o.transpose(1, 0, 2, 3, 4, 5).reshape(B, S, HQ, Dh)


def _fourier_mix(u):
    return jnp.real(jnp.fft.fft2(u.astype(jnp.float32), axes=(1, 3))).astype(u.dtype)


def _token_mixer(h, w_in, g_q, g_k, g_heads, w_out, tables):
    cos_t, sin_t, cos_r, sin_r, cos_c, sin_c = tables
    B, S, _ = h.shape
    proj = h @ w_in
    qa, ka, va, qb, kb, vb, uc = jnp.split(proj, IN_SPLITS, axis=-1)

    def heads(u, n):
        return u.reshape(B, S, n, HEAD_DIM)

    qa = _partial_rope(heads(qa, N_HEADS_A), cos_t, sin_t)
    ka = _partial_rope(heads(ka, N_HEADS_A), cos_t, sin_t)
    oa = _dilated_attention(qa, ka, heads(va, N_HEADS_A))
    qb = _axial_rope(_rms_norm(heads(qb, N_HEADS_B), g_q), cos_r, sin_r, cos_c, sin_c)
    kb = _axial_rope(_rms_norm(heads(kb, N_KV_B), g_k), cos_r, sin_r, cos_c, sin_c)
    ob = _grid_attention(qb, kb, heads(vb, N_KV_B))
    oc = _fourier_mix(heads(uc, N_GROUPS_C))
    o = jnp.concatenate([oa, ob, oc], axis=2)
    o = _rms_norm(o, g_heads.reshape(N_MIX_HEADS, HEAD_DIM))
    return o.reshape(B, S, D_MIX) @ w_out


def _conv_glu_ffn(h, w_gate, w_up, conv_w, conv_b, w_down):
    g = h @ w_gate
    gp = jnp.pad(g, ((0, 0), (1, 1), (0, 0)))
    g = gp[:, :-2] * conv_w[0] + gp[:, 1:-1] * conv_w[1] + gp[:, 2:] * conv_w[2] + conv_b
    return (jax.nn.gelu(g, approximate=True) * (h @ w_up)) @ w_down


def _trunk(x, g_mix_pre, g_mix_post, w_in, g_q, g_k, g_heads, w_out,
           g_ffn_pre, g_ffn_post, w_gate, w_up, conv_w, conv_b, w_down):
    tables = _position_tables(x.shape[1])
    for l in range(DEPTH):
        mix = _token_mixer(_rms_norm(x, g_mix_pre[l]), w_in[l], g_q[l], g_k[l], g_heads[l], w_out[l], tables)
        x = x + _rms_norm(mix, g_mix_post[l])
        ff = _conv_glu_ffn(_rms_norm(x, g_ffn_pre[l]), w_gate[l], w_up[l], conv_w[l], conv_b[l], w_down[l])
        x = x + _rms_norm(ff, g_ffn_post[l])
    return x


def setup_inputs(seed: int = 0) -> dict:
    key = jax.random.key(seed)
    ks = jax.random.split(key, 16)

    def nrm(k, shape, scale):
        return scale * jax.random.normal(k, shape, jnp.float32)

    def gain(k, shape):
        return 1.0 + 0.02 * jax.random.normal(k, shape, jnp.float32)

    return {
        "x_prompt": nrm(ks[0], (BATCH, SEQ, D_MODEL), 1.0),
        "x_sample": nrm(ks[1], (DEC_BATCH, DEC_SEQ, D_MODEL), 1.0),
        "g_mix_pre": gain(ks[2], (DEPTH, D_MODEL)),
        "g_mix_post": gain(ks[3], (DEPTH, D_MODEL)),
        "w_in": nrm(ks[4], (DEPTH, D_MODEL, D_IN), D_MODEL ** -0.5),
        "g_q": gain(ks[5], (DEPTH, HEAD_DIM)),
        "g_k": gain(ks[6], (DEPTH, HEAD_DIM)),
        "g_heads": gain(ks[7], (DEPTH, D_MIX)),
        "w_out": nrm(ks[8], (DEPTH, D_MIX, D_MODEL), D_MIX ** -0.5),
        "g_ffn_pre": gain(ks[9], (DEPTH, D_MODEL)),
        "g_ffn_post": gain(ks[10], (DEPTH, D_MODEL)),
        "w_gate": nrm(ks[11], (DEPTH, D_MODEL, D_FF), D_MODEL ** -0.5),
        "w_up": nrm(ks[12], (DEPTH, D_MODEL, D_FF), D_MODEL ** -0.5),
        "conv_w": nrm(ks[13], (DEPTH, CONV_W, D_FF), CONV_W ** -0.5),
        "conv_b": nrm(ks[14], (DEPTH, D_FF), 0.01),
        "w_down": nrm(ks[15], (DEPTH, D_FF, D_MODEL), D_FF ** -0.5),
    }


def reference(x_prompt, x_sample, g_mix_pre, g_mix_post, w_in, g_q, g_k, g_heads, w_out,
              g_ffn_pre, g_ffn_post, w_gate, w_up, conv_w, conv_b, w_down):
    y_prompt = _trunk(x_prompt, g_mix_pre, g_mix_post, w_in, g_q, g_k, g_heads, w_out,
                      g_ffn_pre, g_ffn_post, w_gate, w_up, conv_w, conv_b, w_down)
    y_sample = _trunk(x_sample, g_mix_pre, g_mix_post, w_in, g_q, g_k, g_heads, w_out,
                      g_ffn_pre, g_ffn_post, w_gate, w_up, conv_w, conv_b, w_down)
    return (y_prompt, y_sample)
```

```python
import numpy as np
import ml_dtypes
from contextlib import ExitStack
import concourse.bass as bass
import concourse.mybir as mybir
from concourse.bass_utils import run_bass_kernel_spmd

F32 = mybir.dt.float32
BF16 = mybir.dt.bfloat16
AF = mybir.ActivationFunctionType
ALU = mybir.AluOpType
AX = mybir.AxisListType
NPBF = ml_dtypes.bfloat16

D = 1024
DFF = 4096
NCOL = 2048
EPS = 1e-6
PADR = 1024


class Sem:
    def __init__(self, nc, name):
        self.h = nc.semaphore(name).__enter__()
        self.count = 0
        self.name = name


class Buf:
    __slots__ = ("w", "r", "ro", "multi", "name", "excl")

    def __init__(self, name="", multi=False):
        self.excl = False
        self.w = {}
        self.r = {}
        self.ro = {}
        self.multi = multi
        self.name = name


class Tile:
    __slots__ = ("t", "b", "lsem", "ssem")

    def __init__(self, t, b):
        self.t = t
        self.b = b
        self.lsem = None
        self.ssem = None


def _b(x):
    return x.b if isinstance(x, Tile) else x


def _merge(dst, src):
    for s, v in src.items():
        if dst.get(s, 0) < v:
            dst[s] = v


class Q:
    def __init__(self, name, eng, sem):
        self.name = name
        self.eng = eng
        self.sem = sem
        self.waited = {}


class FW:
    def __init__(self, nc):
        self.nc = nc
        self.q = {}
        self.sems = []
        for name, eng in (("pe", nc.tensor), ("act", nc.scalar), ("dve", nc.vector),
                          ("pool", nc.gpsimd), ("sp", nc.sync)):
            s = Sem(nc, "q_" + name)
            self.sems.append(s)
            self.q[name] = Q(name, eng, s)
        self.dma_pool = {"l": [], "s": []}
        self.dma_idx = {"l": 0, "s": 0}
        self._rec = None

    def capture(self, fn):
        assert self._rec is None
        self._rec = []
        try:
            ret = fn()
        finally:
            rec, self._rec = self._rec, None
        return rec, ret

    @staticmethod
    def interleave(*lists):
        n = max(len(x) for x in lists)
        for i in range(n):
            for x in lists:
                if i < len(x):
                    x[i]()

    def dma_sem(self, kind="l"):
        pool = self.dma_pool[kind]
        if self.dma_idx[kind] >= len(pool):
            s = Sem(self.nc, "d%s%d" % (kind, len(pool)))
            pool.append(s)
            self.sems.append(s)
        s = pool[self.dma_idx[kind]]
        self.dma_idx[kind] += 1
        return s

    def _wait(self, q, reads, writes):
        deps = {}
        for b in reads:
            b = _b(b)
            _merge(deps, b.w)
            if b.excl:
                for s_, v_ in b.r.items():
                    if s_ is not q.sem and deps.get(s_, 0) < v_:
                        deps[s_] = v_
        for b in writes:
            b = _b(b)
            _merge(deps, b.r)
            _merge(deps, b.ro)
            if not b.multi:
                _merge(deps, b.w)
        for s, v in deps.items():
            if s is q.sem and q.name == "pe":
                continue
            if q.waited.get(s, 0) >= v:
                continue
            q.eng.wait_ge(s.h, v)
            q.waited[s] = v

    def _record(self, s, v, reads, writes):
        for b in reads:
            b = _b(b)
            if b.r.get(s, 0) < v:
                b.r[s] = v
        for b in writes:
            b = _b(b)
            if b.multi:
                if b.r:
                    _merge(b.ro, b.r)
                    b.r = {}
                    b.w = {}
                if b.w.get(s, 0) < v:
                    b.w[s] = v
            else:
                b.w = {s: v}
                b.r = {}

    def op(self, qn, emit, reads=(), writes=(), inc=True):
        if self._rec is not None:
            self._rec.append(lambda: self._op(qn, emit, reads, writes, inc))
            return
        self._op(qn, emit, reads, writes, inc)

    def _op(self, qn, emit, reads=(), writes=(), inc=True):
        q = self.q[qn]
        self._wait(q, reads, writes)
        ins = emit(q.eng)
        if inc:
            q.sem.count += 1
            ins.then_inc(q.sem.h, 1)
            v = q.sem.count
        else:
            v = q.sem.count + 1
        self._record(q.sem, v, reads, writes)

    def mm(self, out, lhsT, rhs, start, stop, reads, writes, last):
        self.op("pe", lambda e: e.matmul(out, lhsT=lhsT, rhs=rhs, start=start, stop=stop),
                reads, writes, inc=last)

    def tr(self, out, in_, ident, reads, writes, last):
        self.op("pe", lambda e: e.transpose(out=out, in_=in_, identity=ident), reads, writes, inc=last)

    def dma(self, qn, out, in_, reads, writes, sem):
        if self._rec is not None:
            self._rec.append(lambda: self._dma(qn, out, in_, reads, writes, sem))
            return
        self._dma(qn, out, in_, reads, writes, sem)

    def _dma(self, qn, out, in_, reads, writes, sem):
        q = self.q[qn]
        self._wait(q, reads, writes)
        q.eng.dma_start(out=out, in_=in_).then_inc(sem.h, 16)
        sem.count += 16
        self._record(sem, sem.count, reads, writes)

    def load(self, tile, src, src_buf, out=None, qn="sp"):
        if tile.lsem is None:
            tile.lsem = self.dma_sem("l" if qn == "sp" else "s")
        self.dma(qn, tile.t[:] if out is None else out, src, [src_buf], [tile], tile.lsem)

    def store(self, dst, dst_buf, tile, src=None, qn="pool"):
        if tile.ssem is None:
            tile.ssem = self.dma_sem("l" if qn == "sp" else "s")
        self.dma(qn, dst, tile.t[:] if src is None else src, [tile], [dst_buf], tile.ssem)

    def barrier(self):
        for q in self.q.values():
            for s in self.sems:
                if s is q.sem:
                    continue
                if s.count > q.waited.get(s, 0):
                    q.eng.wait_ge(s.h, s.count)
                    q.waited[s] = s.count
        self.dma_idx = {"l": 0, "s": 0}


def bc_mid(ap, n):
    a = [list(x) for x in ap.ap]
    return bass.AP(ap.tensor, ap.offset, [a[0], [0, n]] + a[1:])


def bc_last(ap, n):
    a = [list(x) for x in ap.ap]
    return bass.AP(ap.tensor, ap.offset, a + [[0, n]])


def rows_ap(ap2d, row0, rstep, nrows, c0, ncols):
    rs = ap2d.ap[0][0]
    return bass.AP(ap2d.tensor, ap2d.offset + row0 * rs + c0, [[rstep * rs, nrows], [1, ncols]])


class Seq:
    pass


class Builder:
    def __init__(self, seq_lens, depth, taps=False):
        self.nc = nc = bass.Bass("TRN2", target_bir_lowering=False)
        self.fw = FW(nc)
        self.L = L = depth
        self.uid = 0
        self.taps = taps
        self.in_buf = Buf("inputs", multi=True)

        def din(name, shape, dt=F32):
            return nc.dram_tensor(name, list(shape), dt, kind="ExternalInput").ap()

        self.w_in_r = din("w_in_r", [L, D, 1536])
        self.w_ucT = din("w_ucT", [L, 256, D])
        self.w_out = din("w_out", [L, D, D])
        self.w_gate = din("w_gate", [L, D, DFF])
        self.w_up = din("w_up", [L, D, DFF])
        self.w_down = din("w_down", [L, DFF, D])
        self.g_mix_pre = din("g_mix_pre", [L, D])
        self.g_mix_post = din("g_mix_post", [L, D])
        self.g_ffn_pre = din("g_ffn_pre", [L, D])
        self.g_ffn_post = din("g_ffn_post", [L, D])
        self.g_heads = din("g_heads", [L, D])
        self.gqk = din("gqk", [L, 2, 64])
        self.cwb = din("cwb", [L, 128, 4, 32])
        self.c_identf = din("c_identf", [128, 128])
        self.c_identb = din("c_identb", [128, 128], BF16)
        self.c_bd64 = din("c_bd64", [256, 512])
        self.c_masks = din("c_masks", [128, 2, 128], BF16)
        self.c_rhsA = din("c_rhsA", [128, 256], BF16)
        self.c_rhsB = din("c_rhsB", [128, 256], BF16)

        def dsc(name, shape, dt):
            return nc.dram_tensor(name, list(shape), dt).ap(), Buf(name, multi=True)

        self.W_IN, self.W_IN_b = dsc("W_IN", [L, D, NCOL], BF16)
        self.W_OUT, self.W_OUT_b = dsc("W_OUT", [L, D, D], BF16)
        self.WG, self.WG_b = dsc("WG", [L, 32, 128, 8, 128], BF16)
        self.WU, self.WU_b = dsc("WU", [L, 32, 128, 8, 128], BF16)
        self.WD, self.WD_b = dsc("WD", [L, DFF, D], BF16)

        self.seqs = []
        seen = {}
        for si, S in enumerate(seq_lens):
            sq = Seq()
            sq.S = S
            sq.i = si
            sq.x_in = din("x%d" % si, [S, D])
            sq.y_out = nc.dram_tensor("y%d" % si, [S, D], F32, kind="ExternalOutput").ap()
            sq.y_b = Buf("y%d" % si, multi=True)
            if S not in seen:
                seen[S] = (din("ropeA_%d" % S, [S, 32]), din("ropeB_%d" % S, [S, 128]),
                           din("tw_%d" % S, [128, 256]), din("bdcs_%d" % S, [128, 256], BF16))
            sq.ropeA, sq.ropeB, sq.tw, sq.bdcs = seen[S]
            n = "s%d_" % si
            sq.HT_A, sq.HT_A_b = dsc(n + "HTA", [8, 128, S], BF16)
            sq.H2T, sq.H2T_b = dsc(n + "H2T", [8, 128, S + 2], BF16)
            sq.XA, sq.XA_b = dsc(n + "XA", [S, D], F32)
            sq.XB, sq.XB_b = dsc(n + "XB", [S, D], F32)
            sq.QA, sq.QA_b = dsc(n + "QA", [S, 256], BF16)
            sq.KA, sq.KA_b = dsc(n + "KA", [S + 2 * PADR, 256], BF16)
            sq.VA, sq.VA_b = dsc(n + "VA", [S + 2 * PADR, 260], BF16)
            sq.QTB, sq.QTB_b = dsc(n + "QTB", [4, 128, S], BF16)
            sq.KTB, sq.KTB_b = dsc(n + "KTB", [128, S], BF16)
            sq.VB, sq.VB_b = dsc(n + "VB", [S, 130], BF16)
            sq.AB, sq.AB_b = dsc(n + "AB", [S, 512], BF16)
            sq.ACC = []
            sq.ACC_b = []
            for d in (1, 4, 16):
                a, b = dsc(n + "ACC%d" % d, [S, 260], F32)
                sq.ACC.append(a)
                sq.ACC_b.append(b)
            sq.OMIX, sq.OMIX_b = dsc(n + "OMIX", [S, D], BF16)
            self.seqs.append(sq)

        self.build()

    def sb(self, st, shape, dt, name="t"):
        self.uid += 1
        t = st.enter_context(self.nc.sbuf_tensor("%s_%d" % (name, self.uid), list(shape), dt))
        return Tile(t, Buf(name))

    def ps(self, st, shape, dt=F32, name="p"):
        self.uid += 1
        nbytes = int(np.prod(shape[1:])) * (2 if dt == BF16 else 4)
        assert nbytes % 2048 == 0, (name, shape)
        t = st.enter_context(self.nc.psum_tensor("%s_%d" % (name, self.uid), list(shape), dt))
        tl = Tile(t, Buf(name))
        tl.b.excl = True
        return tl

    def ring(self, st, n, shape, dt, name="r", psum=False):
        return [(self.ps if psum else self.sb)(st, shape, dt, name) for _ in range(n)]

    def gain_tile(self, st, src_row):
        t = self.sb(st, [128, D], F32, "gain")
        self.fw.load(t, src_row.broadcast_to([128, D]), self.in_buf)
        return t

    def const_tile(self, st, src, shape, dt):
        t = self.sb(st, shape, dt, "const")
        self.fw.load(t, src, self.in_buf)
        return t

    def eps_tile(self, st):
        t = self.sb(st, [128, 1], F32, "eps")
        self.fw.op("dve", lambda e: e.memset(t.t[:], EPS), [], [t])
        return t

    def rstd(self, ss, out, eps, scale):
        fw = self.fw
        fw.op("act", lambda e: e.activation(out=out.t[:], in_=ss.t[:], func=AF.Ln, bias=eps.t[:, 0:1], scale=scale),
              [ss, eps], [out])
        fw.op("act", lambda e: e.activation(out=out.t[:], in_=out.t[:], func=AF.Exp, scale=-0.5), [out], [out])

    def norm_h_store(self, src_ap, src_t, gnext, W, dstT, dstT_b):
        fw = self.fw
        ss, rs, psT, eps, identb = W["ss"], W["rs"], W["psT"], W["eps"], W["identb"]
        h = W["hs"][W["hi"] % 2]
        hT = W["hTs"][W["hi"] % 2]
        W["hi"] += 1
        fw.op("act", lambda e: e.activation(out=h.t[:], in_=src_ap, func=AF.Square, scale=1.0 / 32.0,
                                            accum_out=ss.t[:, 0:1]), [src_t], [h, ss])
        self.rstd(ss, rs, eps, 1.0)
        fw.op("dve", lambda e: e.scalar_tensor_tensor(out=h.t[:], in0=src_ap, scalar=rs.t[:, 0:1], in1=gnext.t[:],
                                                      op0=ALU.mult, op1=ALU.mult), [src_t, rs, gnext], [h])

        def part2():
            for k in range(8):
                fw.tr(psT.t[:, k, :], h.t[:, k * 128:(k + 1) * 128], identb.t[:], [h, identb], [psT], k == 7)
            fw.op("act", lambda e: e.copy(out=hT.t[:], in_=psT.t[:]), [psT], [hT])
            fw.store(dstT.rearrange("k p t -> p k t"), dstT_b, hT)
        return part2

    def tail(self, y_ap, y_t, xt, gpost, gnext, W, dst_x, dst_x_b, dstT, dstT_b, t32=None):
        fw = self.fw
        ss, rs, eps = W["ss"], W["rs"], W["eps"]
        if t32 is None:
            t32 = W["t32"]
        junk = W["hs"][W["hi"] % 2]
        xn = W["xns"][W["xi"] % 2]
        W["xi"] += 1
        fw.op("act", lambda e: e.activation(out=junk.t[:], in_=y_ap, func=AF.Square, scale=1.0 / 32.0,
                                            accum_out=ss.t[:, 0:1]), [y_t], [junk, ss])
        self.rstd(ss, rs, eps, 1.0)
        fw.op("dve", lambda e: e.scalar_tensor_tensor(out=t32.t[:], in0=y_ap, scalar=rs.t[:, 0:1], in1=gpost.t[:],
                                                      op0=ALU.mult, op1=ALU.mult), [y_t, rs, gpost], [t32])
        fw.op("pool", lambda e: e.tensor_tensor(out=xn.t[:], in0=xt.t[:], in1=t32.t[:], op=ALU.add), [xt, t32], [xn])
        fw.store(dst_x, dst_x_b, xn)
        if gnext is not None:
            return self.norm_h_store(xn.t[:], xn, gnext, W, dstT, dstT_b)
        return None

    def tail_ws(self, st):
        W = {}
        W["ss"] = self.sb(st, [128, 1], F32, "ss")
        W["rs"] = self.sb(st, [128, 1], F32, "rs")
        W["hs"] = self.ring(st, 2, [128, D], BF16, "h")
        W["hTs"] = self.ring(st, 2, [128, 8, 128], BF16, "hT")
        W["hi"] = 0
        W["xi"] = 0
        W["psT"] = self.ps(st, [128, 8, 128], BF16, "psT")
        W["eps"] = self.eps_tile(st)
        W["identb"] = self.const_tile(st, self.c_identb, [128, 128], BF16)
        return W

    def w_items(self):
        L = self.L
        early, late = [], []
        for l in range(L):
            for rt in range(8):
                r = slice(rt * 128, (rt + 1) * 128)
                (early if l == 0 else late).append((self.w_in_r[l, r, :], self.W_IN[l, r, 0:1536], self.W_IN_b, 1536, None))
            for rt in range(8):
                r = slice(rt * 128, (rt + 1) * 128)
                late.append((self.w_out[l, r, :], self.W_OUT[l, r, :], self.W_OUT_b, 1024, None))
                for src, dst, db in ((self.w_gate, self.WG, self.WG_b), (self.w_up, self.WU, self.WU_b)):
                    for ch in range(2):
                        late.append((src[l, r, ch * 2048:(ch + 1) * 2048],
                                     dst[l, ch * 16:(ch + 1) * 16, :, rt, :].rearrange("f p m -> p f m"),
                                     db, 2048, 16))
            for rt in range(32):
                r = slice(rt * 128, (rt + 1) * 128)
                late.append((self.w_down[l, r, :], self.WD[l, r, :], self.WD_b, 1024, None))
        return early, late

    def emit_w_item(self, item, i, s32, s16, engs):
        fw = self.fw
        src, dst, db, C, nf = item
        a = s32[i % len(s32)]
        b = s16[i % len(s16)]
        fw.load(a, src, self.in_buf, out=a.t[:, 0:C])
        en = engs[i % len(engs)]
        if en == "act":
            fw.op("act", lambda e: e.copy(out=b.t[:, 0:C], in_=a.t[:, 0:C]), [a], [b])
        else:
            fw.op(en, lambda e: e.tensor_copy(out=b.t[:, 0:C], in_=a.t[:, 0:C]), [a], [b])
        if nf is None:
            fw.store(dst, db, b, src=b.t[:, 0:C])
        else:
            fw.store(dst, db, b, src=b.t[:, 0:C].rearrange("p (f m) -> p f m", f=nf))

    def phase_W(self):
        fw, nc, L = self.fw, self.nc, self.L
        with ExitStack() as st:
            s32 = self.ring(st, 3, [128, 2048], F32, "s32")
            s16 = self.ring(st, 3, [128, 2048], BF16, "s16")
            early, late = self.w_items()
            self.w_late = late
            for i, it in enumerate(early):
                self.emit_w_item(it, i, s32, s16, ["dve", "act", "pool"])
            bd = self.sb(st, [128, 2, 512], F32, "bd")
            fw.load(bd, self.c_bd64.rearrange("(k p) n -> p k n", p=128), self.in_buf)
            wu = self.ring(st, 2, [128, 2, D], F32, "wucT")
            pf = self.ring(st, 2, [128, 512], F32, "pf", psum=True)
            fo = self.ring(st, 2, [128, 512], BF16, "fo")
            for l in range(L):
                w = wu[l % 2]
                fw.load(w, self.w_ucT[l].rearrange("(k p) n -> p k n", p=128), self.in_buf)
                for dc in range(8):
                    p = pf[dc % 2]
                    o = fo[dc % 2]
                    for kc in range(2):
                        fw.mm(p.t[:], w.t[:, kc, dc * 128:(dc + 1) * 128], bd.t[:, kc, :], kc == 0, kc == 1,
                              [w, bd], [p], kc == 1)
                    fw.op("dve", lambda e: e.tensor_copy(out=o.t[:], in_=p.t[:]), [p], [o])
                    fw.store(self.W_IN[l, dc * 128:(dc + 1) * 128, 1536:2048], self.W_IN_b, o)
        fw.barrier()

    def phase_init(self, sq):
        fw = self.fw
        S = sq.S
        with ExitStack() as st:
            z = self.sb(st, [128, 8, 260], BF16, "zero")
            fw.op("dve", lambda e: e.memset(z.t[:], 0.0), [], [z])
            for base in (0, PADR + S):
                fw.store(sq.KA[base:base + PADR, :].rearrange("(i p) c -> p i c", p=128), sq.KA_b, z,
                         src=z.t[:, :, 0:256])
                fw.store(sq.VA[base:base + PADR, :].rearrange("(i p) c -> p i c", p=128), sq.VA_b, z)
            with self.nc.allow_non_contiguous_dma(reason="halo zero columns"):
                for c0 in (0, S + 1):
                    fw.store(sq.H2T[:, :, c0:c0 + 1].rearrange("k p t -> p k t"), sq.H2T_b, z, src=z.t[:, :, 0:1])
        fw.barrier()

    def phase_0(self, sq):
        fw = self.fw
        with ExitStack() as st:
            W = self.tail_ws(st)
            g = self.gain_tile(st, self.g_mix_pre[0:1, :])
            xs = self.ring(st, 3, [128, D], F32, "x0")
            pend = None
            for ti in range(sq.S // 128):
                xt = xs[ti % 3]
                fw.load(xt, sq.x_in[ti * 128:(ti + 1) * 128, :], self.in_buf)
                p2 = self.norm_h_store(xt.t[:], xt, g, W, sq.HT_A[:, :, ti * 128:(ti + 1) * 128], sq.HT_A_b)
                if pend:
                    pend()
                pend = p2
            pend()
        fw.barrier()

    def phase_A(self, sq, l):
        fw = self.fw
        S = sq.S
        with ExitStack() as st:
            w = self.sb(st, [128, 8, NCOL], BF16, "wA")
            w.b.multi = True
            for k in range(8):
                if w.lsem is None:
                    w.lsem = fw.dma_sem()
                fw.dma("sp", w.t[:, k, :], self.W_IN[l, k * 128:(k + 1) * 128, :], [self.W_IN_b], [w], w.lsem)
            gqk = self.sb(st, [128, 10, 64], F32, "gqk")
            gqk.b.multi = True
            gqk.lsem = fw.dma_sem()
            fw.dma("sp", gqk.t[:, 0:8, :], self.gqk[l:l + 1, 0:1, :].broadcast_to([128, 8, 64]), [self.in_buf], [gqk], gqk.lsem)
            fw.dma("sp", gqk.t[:, 8:10, :], self.gqk[l:l + 1, 1:2, :].broadcast_to([128, 2, 64]), [self.in_buf], [gqk], gqk.lsem)
            eps = self.eps_tile(st)
            identb = self.const_tile(st, self.c_identb, [128, 128], BF16)
            hTs = self.ring(st, 5, [128, 8, 128], BF16, "hTa")
            rAs = self.ring(st, 5, [128, 32], F32, "rA")
            rBs = self.ring(st, 5, [128, 128], F32, "rB")
            banks = self.ring(st, 6, [128, 512], F32, "pA", psum=True)
            psT = self.ps(st, [128, 8, 128], BF16, "psTA")
            qkas = self.ring(st, 2, [128, 512], BF16, "qka")
            tmpa = self.sb(st, [128, 8, 16], F32, "tmpa")
            tmpb = self.sb(st, [128, 8, 16], F32, "tmpb")
            def nr_scratch(nh):
                return (self.sb(st, [128, nh, 64], F32, "sqj"), self.sb(st, [128, nh], F32, "ssb"),
                        self.sb(st, [128, nh], F32, "rsb"), self.sb(st, [128, nh, 64], F32, "tn"),
                        self.sb(st, [128, nh, 64], F32, "ra"), self.sb(st, [128, nh, 64], F32, "rb"))
            scq = nr_scratch(8)
            sck = nr_scratch(2)
            qqs = self.ring(st, 2, [128, 512], BF16, "qq")
            kks = self.ring(st, 2, [128, 128], BF16, "kk")
            qkT = self.ring(st, 2, [128, 5, 128], BF16, "qkT")
            vas = self.ring(st, 2, [128, 4, 65], BF16, "va")
            vbs = self.ring(st, 2, [128, 2, 65], BF16, "vb")
            abs_ = self.ring(st, 2, [128, 512], BF16, "ab")
            for v in vas + vbs:
                fw.op("pool", lambda e: e.memset(v.t[:], 1.0), [], [v])
            pendA = None

            def loadsA(tj):
                tt = tj * 128
                fw.load(hTs[tj % 5], sq.HT_A[:, :, tt:tt + 128].rearrange("k p t -> p k t"), sq.HT_A_b)
                fw.load(rAs[tj % 5], sq.ropeA[tt:tt + 128, :], self.in_buf)
                fw.load(rBs[tj % 5], sq.ropeB[tt:tt + 128, :], self.in_buf)

            for tj in range(min(3, S // 128)):
                loadsA(tj)
            for ti in range(S // 128):
                t0 = ti * 128
                hT = hTs[ti % 5]
                rA = rAs[ti % 5]
                rB = rBs[ti % 5]
                if ti + 3 < S // 128:
                    loadsA(ti + 3)
                bk = [banks[(ti * 4 + nb) % 6] for nb in range(4)]
                for nb in range(4):
                    for k in range(8):
                        fw.mm(bk[nb].t[:], hT.t[:, k, :], w.t[:, k, nb * 512:(nb + 1) * 512], k == 0, k == 7,
                              [hT, w], [bk[nb]], k == 7)
                p0, p1, p2, p3 = bk
                qka = qkas[ti % 2]
                qq, kk = qqs[ti % 2], kks[ti % 2]
                va, vb, ab = vas[ti % 2], vbs[ti % 2], abs_[ti % 2]

                def chain0(p0=p0, qka=qka, rA=rA, t0=t0):
                    fw.op("act", lambda e: e.copy(out=qka.t[:], in_=p0.t[:]), [p0], [qka])
                    pv = p0.t[:].rearrange("p (h d) -> p h d", d=64)
                    ov = qka.t[:].rearrange("p (h d) -> p h d", d=64)
                    fw.op("dve", lambda e: e.tensor_tensor(out=tmpa.t[:], in0=pv[:, :, 0:16], in1=bc_mid(rA.t[:, 0:16], 8),
                                                           op=ALU.mult), [p0, rA], [tmpa])
                    fw.op("dve", lambda e: e.tensor_tensor(out=tmpb.t[:, :, 0:8], in0=pv[:, :, 8:16],
                                                           in1=bc_mid(rA.t[:, 16:24], 8), op=ALU.mult), [p0, rA], [tmpb])
                    fw.op("dve", lambda e: e.tensor_tensor(out=tmpb.t[:, :, 8:16], in0=pv[:, :, 0:8],
                                                           in1=bc_mid(rA.t[:, 24:32], 8), op=ALU.mult), [p0, rA], [tmpb])
                    fw.op("dve", lambda e: e.tensor_tensor(out=ov[:, :, 0:16], in0=tmpa.t[:], in1=tmpb.t[:], op=ALU.add),
                          [tmpa, tmpb], [qka])
                    fw.store(sq.QA[t0:t0 + 128, :], sq.QA_b, qka, src=qka.t[:, 0:256], qn="sp")
                    fw.store(sq.KA[PADR + t0:PADR + t0 + 128, :], sq.KA_b, qka, src=qka.t[:, 256:512], qn="sp")

                def chain_nr(srct, src, h0, nh, sc, dst, rB=rB):
                    sqj, ssb, rsb, tn, ra, rb = sc
                    gs = slice(h0, h0 + nh)
                    sv = src.rearrange("p (h d) -> p h d", d=64)
                    fw.op("act", lambda e: e.activation(out=sqj.t[:], in_=sv, func=AF.Square), [srct], [sqj])
                    fw.op("dve", lambda e: e.tensor_reduce(out=ssb.t[:], in_=sqj.t[:], axis=AX.X, op=ALU.add), [sqj], [ssb])
                    fw.op("act", lambda e: e.activation(out=rsb.t[:], in_=ssb.t[:], func=AF.Ln,
                                                        bias=eps.t[:, 0:1], scale=1.0 / 64.0), [ssb, eps], [rsb])
                    fw.op("act", lambda e: e.activation(out=rsb.t[:], in_=rsb.t[:], func=AF.Exp, scale=-0.5), [rsb], [rsb])
                    fw.op("dve", lambda e: e.tensor_tensor(out=tn.t[:], in0=sv, in1=bc_last(rsb.t[:], 64), op=ALU.mult),
                          [srct, rsb], [tn])
                    fw.op("pool", lambda e: e.tensor_tensor(out=tn.t[:], in0=tn.t[:], in1=gqk.t[:, gs, :], op=ALU.mult),
                          [tn, gqk], [tn])
                    fw.op("dve", lambda e: e.tensor_tensor(out=ra.t[:], in0=tn.t[:], in1=bc_mid(rB.t[:, 0:64], nh),
                                                           op=ALU.mult), [tn, rB], [ra])
                    t5 = tn.t[:].rearrange("p h (b x d) -> p h b x d", b=2, x=2)
                    r5 = rb.t[:].rearrange("p h (b x d) -> p h b x d", b=2, x=2)
                    s4 = rB.t[:, 64:128].rearrange("p (b x d) -> p b x d", b=2, x=2)
                    fw.op("pool", lambda e: e.tensor_tensor(out=r5[:, :, :, 0, :], in0=t5[:, :, :, 1, :],
                                                            in1=bc_mid(s4[:, :, 0, :], nh), op=ALU.mult), [tn, rB], [rb])
                    fw.op("pool", lambda e: e.tensor_tensor(out=r5[:, :, :, 1, :], in0=t5[:, :, :, 0, :],
                                                            in1=bc_mid(s4[:, :, 1, :], nh), op=ALU.mult), [tn, rB], [rb])
                    fw.op("dve", lambda e: e.tensor_tensor(out=dst.t[:].rearrange("p (h d) -> p h d", d=64),
                                                           in0=ra.t[:], in1=rb.t[:], op=ALU.add), [ra, rb], [dst])

                def chainv(p2=p2, p3=p3, va=va, vb=vb, ab=ab, t0=t0):
                    fw.op("act", lambda e: e.copy(out=va.t[:, :, 0:64], in_=p2.t[:, 128:384].rearrange("p (h d) -> p h d", d=64)),
                          [p2], [va])
                    fw.op("act", lambda e: e.copy(out=vb.t[:, :, 0:64], in_=p2.t[:, 384:512].rearrange("p (h d) -> p h d", d=64)),
                          [p2], [vb])
                    fw.store(sq.VA[PADR + t0:PADR + t0 + 128, :], sq.VA_b, va, src=va.t[:].rearrange("p h d -> p (h d)"), qn="sp")
                    fw.store(sq.VB[t0:t0 + 128, :], sq.VB_b, vb, src=vb.t[:].rearrange("p h d -> p (h d)"), qn="sp")
                    fw.op("dve", lambda e: e.tensor_copy(out=ab.t[:], in_=p3.t[:]), [p3], [ab])
                    fw.store(sq.AB[t0:t0 + 128, :], sq.AB_b, ab, qn="sp")

                c0, _ = fw.capture(chain0)
                cq, _ = fw.capture(lambda: chain_nr(p1, p1.t[:, :], 0, 8, scq, qq))
                ck, _ = fw.capture(lambda: chain_nr(p2, p2.t[:, 0:128], 8, 2, sck, kk))
                cv, _ = fw.capture(chainv)
                fw.interleave(cq, c0, ck, cv)
                if pendA:
                    pendA()

                def mk_tr(qq=qq, kk=kk, ti=ti, t0=t0):
                    def f():
                        for c in range(4):
                            fw.tr(psT.t[:, c, :], qq.t[:, c * 128:(c + 1) * 128], identb.t[:], [qq, identb], [psT], False)
                        fw.tr(psT.t[:, 4, :], kk.t[:], identb.t[:], [kk, identb], [psT], True)
                        qT = qkT[ti % 2]
                        fw.op("act", lambda e: e.copy(out=qT.t[:], in_=psT.t[:, 0:5, :]), [psT], [qT])
                        fw.store(sq.QTB[:, :, t0:t0 + 128].rearrange("r p t -> p r t"), sq.QTB_b, qT, src=qT.t[:, 0:4, :], qn="sp")
                        fw.store(sq.KTB[:, t0:t0 + 128], sq.KTB_b, qT, src=qT.t[:, 4, :], qn="sp")
                    return f
                pendA = mk_tr()
            pendA()
        fw.barrier()

    def phase_B1(self, sq, l):
        fw = self.fw
        S = sq.S
        with ExitStack() as st:
            masks = self.const_tile(st, self.c_masks, [128, 2, 128], BF16)
            identb = self.const_tile(st, self.c_identb, [128, 128], BF16)
            Qts = self.ring(st, 5, [128, 256], BF16, "Qt")
            Kts = self.ring(st, 6, [128, 256], BF16, "Kt")
            Vts = self.ring(st, 10, [128, 260], BF16, "Vt")
            QTs = self.ring(st, 3, [128, 2, 128], BF16, "QT")
            KTs = self.ring(st, 8, [128, 2, 128], BF16, "KT")
            Pes = self.ring(st, 3, [128, 2, 256], BF16, "Pe")
            Pms = self.ring(st, 6, [128, 2, 2, 128], BF16, "Pm")
            Osbs = self.ring(st, 3, [128, 260], F32, "Osb")
            psTs = self.ring(st, 2, [128, 8, 128], BF16, "psTb", psum=True)
            psSs = self.ring(st, 2, [128, 2, 512], F32, "psS", psum=True)
            psOs = self.ring(st, 2, [128, 512], F32, "psO", psum=True)
            cnt = {"k": 0, "q": 0, "s": 0, "t": 0, "o": 0}
            blocks = []
            for di, d in enumerate((1, 4, 16)):
                nb = (S // d) // 128
                for r in range(d):
                    for b in range(nb):
                        blocks.append((di, d, r, b))
            state = {}
            ktiles = {}

            kl = {}
            ql = {}

            def load_k0(d, r, j):
                i = cnt["k"]
                cnt["k"] += 1
                Kt, Vt = Kts[i % 6], Vts[i % 10]
                row0 = PADR + d * (128 * j - 64) + r
                fw.load(Kt, rows_ap(sq.KA, row0, d, 128, 0, 256), sq.KA_b)
                fw.load(Vt, rows_ap(sq.VA, row0, d, 128, 0, 260), sq.VA_b)
                kl[(d, r, j)] = (Kt, Vt, KTs[i % 8])

            def S0(bi):
                di, d, r, b = blocks[bi]
                if b == 0:
                    load_k0(d, r, 0)
                load_k0(d, r, b + 1)
                qi = cnt["q"]
                cnt["q"] += 1
                Qt = Qts[qi % 5]
                qrow0 = d * 128 * b + r
                fw.load(Qt, rows_ap(sq.QA, qrow0, d, 128, 0, 256), sq.QA_b)
                ql[bi] = (Qt, QTs[qi % 3], qrow0)

            def tr_k(d, r, j):
                Kt, Vt, KT = kl.pop((d, r, j))
                pT = psTs[cnt["t"] % 2]
                cnt["t"] += 1
                for c in range(2):
                    fw.tr(pT.t[:, c, :], Kt.t[:, c * 128:(c + 1) * 128], identb.t[:], [Kt, identb], [pT], c == 1)
                fw.op("dve", lambda e: e.tensor_copy(out=KT.t[:], in_=pT.t[:, 0:2, :]), [pT], [KT])
                ktiles[(d, r, j)] = (KT, Vt)

            def S1(bi):
                di, d, r, b = blocks[bi]
                if b == 0:
                    tr_k(d, r, 0)
                tr_k(d, r, b + 1)
                Qt, QT, qrow0 = ql.pop(bi)
                pT = psTs[cnt["t"] % 2]
                cnt["t"] += 1
                for c in range(2):
                    fw.tr(pT.t[:, c, :], Qt.t[:, c * 128:(c + 1) * 128], identb.t[:], [Qt, identb], [pT], c == 1)
                fw.op("dve", lambda e: e.tensor_copy(out=QT.t[:], in_=pT.t[:, 0:2, :]), [pT], [QT])
                state[bi] = {"QT": QT, "k": [ktiles[(d, r, b)], ktiles[(d, r, b + 1)]], "qrow0": qrow0}
                del ktiles[(d, r, b)]

            def S2(bi):
                stt = state[bi]
                QT = stt["QT"]
                pms = []
                for jj in range(2):
                    KT, Vt = stt["k"][jj]
                    si = cnt["s"]
                    cnt["s"] += 1
                    pS, Pe, Pm = psSs[si % 2], Pes[si % 3], Pms[si % 6]
                    for h in range(4):
                        c, hp = h // 2, h % 2
                        po = hp * 64
                        fw.mm(pS.t[:, hp, c * 128:(c + 1) * 128], KT.t[po:po + 64, c, :], QT.t[po:po + 64, c, :],
                              True, True, [KT, QT], [pS], h == 3)
                    fw.op("act", lambda e: e.activation(out=Pe.t[:], in_=pS.t[:, :, 0:256], func=AF.Exp, scale=0.125),
                          [pS], [Pe])
                    mk = bass.AP(masks.t[:].tensor, masks.t[:, jj, :].offset,
                                 [list(masks.t[:].ap[0]), [0, 2], [0, 2], [1, 128]])
                    fw.op("dve",
                          lambda e: e.tensor_tensor(out=Pm.t[:], in0=Pe.t[:].rearrange("p a (c q) -> p a c q", c=2),
                                                    in1=mk, op=ALU.mult), [Pe, masks], [Pm])
                    pms.append((Pm, Vt))
                stt["pms"] = pms

            def S3(bi):
                di, d, r, b = blocks[bi]
                stt = state.pop(bi)
                oi = cnt["o"]
                cnt["o"] += 1
                pO = psOs[oi % 2]
                for h in range(4):
                    c, hp = h // 2, h % 2
                    for jj in range(2):
                        Pm, Vt = stt["pms"][jj]
                        fw.mm(pO.t[:, h * 65:(h + 1) * 65], Pm.t[:, hp, c, :], Vt.t[:, h * 65:(h + 1) * 65],
                              jj == 0, jj == 1, [Pm, Vt], [pO], h == 3 and jj == 1)
                Osb = Osbs[oi % 3]
                fw.op("act", lambda e: e.copy(out=Osb.t[:], in_=pO.t[:, 0:260]), [pO], [Osb])
                fw.store(rows_ap(sq.ACC[di], stt["qrow0"], d, 128, 0, 260), sq.ACC_b[di], Osb)

            n = len(blocks)
            S0(0)
            if n > 1:
                S0(1)
            for step in range(n + 2):
                if step + 2 < n:
                    S0(step + 2)
                if step < n:
                    S1(step)
                if 1 <= step <= n:
                    S2(step - 1)
                if step >= 2:
                    S3(step - 2)
        fw.barrier()

    def phase_B2(self, sq, l):
        fw = self.fw
        S = sq.S
        nkb = S // 128
        nqg = S // 512
        with ExitStack() as st:
            KT = self.sb(st, [128, S], BF16, "KTb")
            fw.load(KT, sq.KTB, sq.KTB_b)
            VB = self.sb(st, [128, nkb, 130], BF16, "VBb")
            fw.load(VB, sq.VB.rearrange("(k p) c -> p k c", p=128), sq.VB_b)
            identf = self.const_tile(st, self.c_identf, [128, 128], F32)
            Qgs = self.ring(st, 3, [128, 512], BF16, "Qg")
            Pts = self.ring(st, 4, [128, 2, 512], BF16, "Pt")
            Osbs = self.ring(st, 2, [128, 2, 512], F32, "OsbB")
            rdens = self.ring(st, 2, [128, 4, 1], F32, "rden")
            obs = self.ring(st, 3, [128, 4, 64], BF16, "ob")
            psSs = self.ring(st, 3, [128, 2, 512], F32, "psSB", psum=True)
            psO = self.ring(st, 2, [128, 512], F32, "psOB", psum=True)
            bg = list(getattr(self, "w_late", []))
            self.w_late = []
            if bg:
                bg32 = self.ring(st, 3, [128, 2048], F32, "bg32")
                bg16 = self.ring(st, 3, [128, 2048], BF16, "bg16")
            bgi = [0]

            def bg_step():
                if bgi[0] < len(bg):
                    self.emit_w_item(bg[bgi[0]], bgi[0], bg32, bg16, ["dve"])
                    bgi[0] += 1
            groups = [(r, qg) for r in range(4) for qg in range(nqg)]
            iters = [(gi, kb) for gi in range(len(groups)) for kb in range(nkb)]
            cnt = {"u": 0, "ps": 0}

            def load_q(gi):
                r, qg = groups[gi]
                fw.load(Qgs[gi % 3], sq.QTB[r, :, qg * 512:(qg + 1) * 512], sq.QTB_b)

            def emit_qk(i):
                gi, kb = iters[i]
                if kb == 0 and gi + 1 < len(groups):
                    load_q(gi + 1)
                Qg = Qgs[gi % 3]
                ps = psSs[cnt["ps"] % 3]
                cnt["ps"] += 1
                P = Pts[i % 4]
                ks = slice(kb * 128, (kb + 1) * 128)
                fw.mm(ps.t[:, 0, :], KT.t[0:64, ks], Qg.t[0:64, :], True, True, [KT, Qg], [ps], False)
                fw.mm(ps.t[:, 1, :], KT.t[64:128, ks], Qg.t[64:128, :], True, True, [KT, Qg], [ps], True)
                fw.op("act", lambda e: e.activation(out=P.t[:], in_=ps.t[:], func=AF.Exp, scale=0.125), [ps], [P])

            def emit_pv(i):
                gi, kb = iters[i]
                P = Pts[i % 4]
                for g in range(2):
                    fw.mm(psO[g].t[0:65, :], VB.t[:, kb, g * 65:(g + 1) * 65], P.t[:, g, :], kb == 0, kb == nkb - 1,
                          [VB, P], [psO[g]], g == 1)

            def fin1(gi):
                Osb = Osbs[gi % 2]
                for g in range(2):
                    fw.op("dve", lambda e: e.tensor_copy(out=Osb.t[0:65, g, :], in_=psO[g].t[0:65, :]), [psO[g]], [Osb])

            def fin2(gi):
                r, qg = groups[gi]
                q0 = qg * 512
                Osb = Osbs[gi % 2]
                psTt = psSs[cnt["ps"] % 3]
                cnt["ps"] += 1
                for g in range(2):
                    psTv = psTt.t[:, g, 0:260].rearrange("p (q d) -> p q d", d=65)
                    for qt in range(4):
                        fw.tr(psTv[:, qt, :], Osb.t[0:65, g, qt * 128:(qt + 1) * 128], identf.t[0:65, 0:65],
                              [Osb, identf], [psTt], qt == 3)
                    u = cnt["u"]
                    cnt["u"] += 1
                    rden = rdens[u % 2]
                    ob = obs[u % 3]
                    fw.op("dve", lambda e: e.reciprocal(out=rden.t[:], in_=psTv[:, :, 64:65]), [psTt], [rden])
                    rd_b = bass.AP(rden.t[:].tensor, rden.t[:].offset, [list(rden.t[:].ap[0]), [1, 4], [0, 64]])
                    fw.op("dve", lambda e: e.tensor_tensor(out=ob.t[:], in0=psTv[:, :, 0:64], in1=rd_b, op=ALU.mult),
                          [psTt, rden], [ob])
                    col0 = 256 + (g * 4 + r) * 64
                    dst = bass.AP(sq.OMIX.tensor, sq.OMIX.offset + q0 * D + col0, [[D, 128], [128 * D, 4], [1, 64]])
                    fw.store(dst, sq.OMIX_b, ob)

            load_q(0)
            deferred = []
            n = len(iters)
            for step in range(n + 1):
                if step < n:
                    emit_qk(step)
                if step >= 1:
                    i = step - 1
                    emit_pv(i)
                    gi, kb = iters[i]
                    if kb == nkb - 1:
                        fin1(gi)
                        deferred.append((step + 3, gi))
                while deferred and deferred[0][0] <= step:
                    fin2(deferred.pop(0)[1])
                if step % 12 == 5:
                    bg_step()
            for _, gi in deferred:
                fin2(gi)
            while bgi[0] < len(bg):
                bg_step()
        fw.barrier()

    def phase_C(self, sq, l):
        fw = self.fw
        S = sq.S
        J = S // 128
        ncols = 128 // J
        ngroups = 256 // ncols
        with ExitStack() as st:
            self.uid += 1
            X2t = st.enter_context(self.nc.sbuf_tensor("X2_%d" % self.uid, [128, 512, J], BF16))
            X2 = [Tile(X2t, Buf("X2_%d" % i)) for i in range(8)]
            Xcs = self.ring(st, 2, [128, J, 64], BF16, "Xc")
            xv = sq.AB.rearrange("(p j) c -> p j c", j=J)
            for ci in range(8):
                Xc = Xcs[ci % 2]
                fw.load(Xc, xv[:, :, ci * 64:(ci + 1) * 64], sq.AB_b)
                en = ("dve", "act", "pool")[ci % 3]
                src = Xc.t[:].rearrange("p j c -> p c j")
                dstv = X2t[:, ci * 64:(ci + 1) * 64, :]
                if en == "act":
                    fw.op("act", lambda e: e.copy(out=dstv, in_=src), [Xc], [X2[ci]])
                else:
                    fw.op(en, lambda e: e.tensor_copy(out=dstv, in_=src), [Xc], [X2[ci]])
            OC = self.sb(st, [128, J, 256], BF16, "OC")
            rhsA = self.const_tile(st, self.c_rhsA, [128, 256], BF16)
            rhsB = self.const_tile(st, self.c_rhsB, [128, 256], BF16)
            tw = self.const_tile(st, sq.tw, [128, 256], F32)
            bdcs = self.const_tile(st, sq.bdcs, [128, 256], BF16)
            ps1 = self.ring(st, 3, [128, 512], F32, "ps1", psum=True)
            ps2 = self.ring(st, 2, [128, 512], F32, "ps2", psum=True)
            t1s = self.ring(st, 3, [128, 2, 128], F32, "t1")
            t2s = self.ring(st, 3, [128, 2, 128], F32, "t2")
            Zs = self.ring(st, 4, [128, 2, 128], BF16, "Z")
            OCb = [Buf("OC%d" % i) for i in range(ngroups // 4)]

            def T1(gi):
                c0 = gi * ncols
                lhsA = X2t[:, c0:c0 + ncols, :].rearrange("p c j -> p (c j)")
                lhsB = X2t[:, 256 + c0:256 + c0 + ncols, :].rearrange("p c j -> p (c j)")
                XA_, XB_ = X2[c0 // 64], X2[(256 + c0) // 64]
                p1 = ps1[gi % 3]
                fw.mm(p1.t[:, 0:256], lhsA, rhsA.t[:], True, False, [XA_, rhsA], [p1], False)
                fw.mm(p1.t[:, 0:256], lhsB, rhsB.t[:], False, True, [XB_, rhsB], [p1], True)

            def T2(gi):
                p1 = ps1[gi % 3]
                t1, t2, Z = t1s[gi % 3], t2s[gi % 3], Zs[gi % 4]
                pv = p1.t[:, 0:256].rearrange("p (a f) -> p a f", a=2)
                fw.op("dve", lambda e: e.tensor_tensor(out=t1.t[:], in0=pv, in1=bc_mid(tw.t[:, 0:128], 2), op=ALU.mult),
                      [p1, tw], [t1])
                fw.op("dve", lambda e: e.tensor_tensor(out=t2.t[:], in0=pv, in1=bc_mid(tw.t[:, 128:256], 2), op=ALU.mult),
                      [p1, tw], [t2])
                fw.op("pool", lambda e: e.tensor_tensor(out=Z.t[:, 0, :], in0=t1.t[:, 0, :], in1=t2.t[:, 1, :], op=ALU.add),
                      [t1, t2], [Z])
                fw.op("pool", lambda e: e.tensor_tensor(out=Z.t[:, 1, :], in0=t1.t[:, 1, :], in1=t2.t[:, 0, :],
                                                        op=ALU.subtract), [t1, t2], [Z])

            def T3(gi):
                gq, gg = gi // 4, gi % 4
                p2 = ps2[gq % 2]
                Z = Zs[gi % 4]
                fw.mm(p2.t[:, gg * 128:(gg + 1) * 128], Z.t[:, 0, :], bdcs.t[:, 0:128], True, False, [Z, bdcs], [p2], False)
                fw.mm(p2.t[:, gg * 128:(gg + 1) * 128], Z.t[:, 1, :], bdcs.t[:, 128:256], False, True, [Z, bdcs], [p2], True)
                if gg == 3:
                    cb = gq * 4 * ncols
                    out_ap = OC.t[:, :, cb:cb + 4 * ncols].rearrange("p f c -> p c f")
                    in_ap = p2.t[:].rearrange("p (c f) -> p c f", f=J)
                    fw.op("act", lambda e: e.copy(out=out_ap, in_=in_ap), [p2], [OCb[gq]])

            for step in range(ngroups + 2):
                if step < ngroups:
                    T1(step)
                if 1 <= step <= ngroups:
                    T2(step - 1)
                if step >= 2:
                    T3(step - 2)
            OC.ssem = fw.dma_sem("s")
            dst = bass.AP(sq.OMIX.tensor, sq.OMIX.offset + 768, [[D, 128], [128 * D, J], [1, 256]])
            fw.dma("pool", dst, OC.t[:], OCb, [sq.OMIX_b], OC.ssem)
        fw.barrier()

    def phase_D(self, sq, l, x_src, x_src_b, x_dst, x_dst_b):
        fw = self.fw
        S = sq.S
        nt = S // 128
        with ExitStack() as st:
            Ws = [self.tail_ws(st) for _ in range(2)]
            for W in Ws:
                W["xns"] = self.ring(st, 2, [128, D], F32, "xn")
                W["t32"] = self.sb(st, [128, D], F32, "t32")
            eps = Ws[0]["eps"]
            identb = Ws[0]["identb"]
            wo = self.sb(st, [128, 8, D], BF16, "wo")
            fw.load(wo, self.W_OUT[l].rearrange("(k p) n -> p k n", p=128), self.W_OUT_b)
            gh8 = self.sb(st, [128, 8], F32, "gh8")
            with self.nc.allow_non_contiguous_dma(reason="tiny gain vector relayout"):
                fw.load(gh8, self.g_heads[l].rearrange("(k p) -> p k", p=128), self.in_buf)
            for k in range(8):
                fw.op("dve", lambda e: e.tensor_scalar(out=wo.t[:, k, :], in0=wo.t[:, k, :], scalar1=gh8.t[:, k:k + 1],
                                                       scalar2=None, op0=ALU.mult), [wo, gh8], [wo])
            gpost = self.gain_tile(st, self.g_mix_post[l:l + 1, :])
            gnext = self.gain_tile(st, self.g_ffn_pre[l:l + 1, :])
            oms = self.ring(st, 6, [128, D], BF16, "om")
            accs = [self.ring(st, 6, [128, 4, 65], F32, "acc%d" % i) for i in range(3)]
            xts = self.ring(st, 8, [128, D], F32, "xt")
            o32a = self.ring(st, 2, [128, 4, 64], F32, "o32a")
            junk = self.ring(st, 2, [128, 16, 64], F32, "junk")
            ss16 = self.ring(st, 2, [128, 16], F32, "ss16")
            rs16 = self.ring(st, 2, [128, 16], F32, "rs16")
            rden = self.ring(st, 2, [128, 4, 1], F32, "rdenD")
            obs = self.ring(st, 4, [128, D], BF16, "obD")
            oT = self.ring(st, 2, [128, 8, 128], BF16, "oT")
            psT2 = self.ring(st, 2, [128, 8, 128], BF16, "psT2", psum=True)
            psY = self.ring(st, 2, [128, D], F32, "psY", psum=True)

            def loads(ti):
                rs_ = slice(ti * 128, ti * 128 + 128)
                fw.load(oms[ti % 6], sq.OMIX[rs_, 256:1024], sq.OMIX_b, out=oms[ti % 6].t[:, 256:1024])
                for i in range(3):
                    fw.load(accs[i][ti % 6], sq.ACC[i][rs_, :].rearrange("p (h d) -> p h d", d=65), sq.ACC_b[i])
                fw.load(xts[ti % 8], x_src[rs_, :], x_src_b)

            def stageA(ti):
                c = ti % 2
                om = oms[ti % 6]
                a = [accs[i][ti % 6] for i in range(3)]
                ob = obs[ti % 4]
                oa, jk, s16, r16, rd = o32a[c], junk[c], ss16[c], rs16[c], rden[c]
                fw.op("pool", lambda e: e.tensor_tensor(out=a[0].t[:], in0=a[0].t[:], in1=a[1].t[:], op=ALU.add), [a[0], a[1]], [a[0]])
                fw.op("pool", lambda e: e.tensor_tensor(out=a[0].t[:], in0=a[0].t[:], in1=a[2].t[:], op=ALU.add), [a[0], a[2]], [a[0]])
                fw.op("dve", lambda e: e.reciprocal(out=rd.t[:], in_=a[0].t[:, :, 64:65]), [a[0]], [rd])
                rd_b = bass.AP(rd.t[:].tensor, rd.t[:].offset, [list(rd.t[:].ap[0]), [1, 4], [0, 64]])
                fw.op("dve", lambda e: e.tensor_tensor(out=oa.t[:], in0=a[0].t[:, :, 0:64], in1=rd_b, op=ALU.mult),
                      [a[0], rd], [oa])
                fw.op("act", lambda e: e.activation(out=jk.t[:, 0:4, :], in_=oa.t[:], func=AF.Square), [oa], [jk])
                fw.op("act", lambda e: e.activation(out=jk.t[:, 4:16, :],
                                                    in_=om.t[:, 256:1024].rearrange("p (h d) -> p h d", d=64),
                                                    func=AF.Square), [om], [jk])
                fw.op("dve", lambda e: e.tensor_reduce(out=s16.t[:], in_=jk.t[:], axis=AX.X, op=ALU.add), [jk], [s16])
                self.rstd(s16, r16, eps, 1.0 / 64.0)
                fw.op("dve", lambda e: e.tensor_tensor(out=ob.t[:, 0:256].rearrange("p (h d) -> p h d", d=64), in0=oa.t[:],
                                                       in1=bc_last(r16.t[:, 0:4], 64), op=ALU.mult), [oa, r16], [ob])
                fw.op("dve", lambda e: e.tensor_tensor(out=ob.t[:, 256:1024].rearrange("p (h d) -> p h d", d=64),
                                                       in0=om.t[:, 256:1024].rearrange("p (h d) -> p h d", d=64),
                                                       in1=bc_last(r16.t[:, 4:16], 64), op=ALU.mult), [om, r16], [ob])

            def stageB(ti):
                ob = obs[ti % 4]
                pT = psT2[ti % 2]
                for k in range(8):
                    fw.tr(pT.t[:, k, :], ob.t[:, k * 128:(k + 1) * 128], identb.t[:], [ob, identb], [pT], k == 7)
                o_T = oT[ti % 2]
                fw.op("dve", lambda e: e.tensor_copy(out=o_T.t[:], in_=pT.t[:]), [pT], [o_T])
                pY = psY[ti % 2]
                for half in range(2):
                    hs = slice(half * 512, (half + 1) * 512)
                    for k in range(8):
                        fw.mm(pY.t[:, hs], o_T.t[:, k, :], wo.t[:, k, hs], k == 0, k == 7, [o_T, wo], [pY],
                              k == 7 and half == 1)

            def do_tail(ti):
                t0 = ti * 128
                pY = psY[ti % 2]
                return self.tail(pY.t[:], pY, xts[ti % 8], gpost, gnext, Ws[ti % 2], x_dst[t0:t0 + 128, :], x_dst_b,
                                 sq.H2T[:, :, 1 + t0:1 + t0 + 128], sq.H2T_b)

            for t in range(min(4, nt)):
                loads(t)
            l0, _ = fw.capture(lambda: stageA(0))
            l1, _ = fw.capture(lambda: stageA(1))
            fw.interleave(l0, l1)
            pends = []
            for p in range(nt // 2 + 1):
                for t in (2 * p + 4, 2 * p + 5):
                    if t < nt:
                        loads(t)
                lists = []
                newp = []
                for t in (2 * p + 2, 2 * p + 3):
                    if t < nt:
                        lst, _ = fw.capture(lambda t=t: stageA(t))
                        lists.append(lst)
                for t in (2 * p - 2, 2 * p - 1):
                    if 0 <= t < nt:
                        lst, pd = fw.capture(lambda t=t: do_tail(t))
                        lists.append(lst)
                        if pd:
                            newp.append(pd)
                if lists:
                    fw.interleave(*lists)
                for t in (2 * p, 2 * p + 1):
                    if t < nt:
                        stageB(t)
                for pd in pends:
                    pd()
                pends = newp
            for pd in pends:
                pd()
        fw.barrier()

    def phase_E(self, sq, l, x_src, x_src_b, x_dst, x_dst_b, last):
        fw = self.fw
        S = sq.S
        G = 512
        with ExitStack() as st:
            W = self.tail_ws(st)
            W["xns"] = self.ring(st, 2, [128, D], F32, "xnE")
            ysbs = self.ring(st, 2, [128, D], F32, "ysb")
            wd = self.sb(st, [128, 32, D], BF16, "wd")
            wd.b.multi = True
            wd.lsem = fw.dma_sem()
            wdv = self.WD[l].rearrange("(f p) n -> p f n", p=128)
            for i in range(4):
                fw.dma("sp", wd.t[:, i * 8:(i + 1) * 8, :], wdv[:, i * 8:(i + 1) * 8, :], [self.WD_b], [wd], wd.lsem)
            cwb = self.const_tile(st, self.cwb[l], [128, 4, 32], F32)
            gpost = self.gain_tile(st, self.g_ffn_post[l:l + 1, :])
            gnext = None if last else self.gain_tile(st, self.g_mix_pre[l + 1:l + 2, :])
            h2s = self.ring(st, 2, [128, 8, G + 2], BF16, "h2")
            wgs = self.ring(st, 3, [128, 8, 128], BF16, "wg")
            wus = self.ring(st, 3, [128, 8, 128], BF16, "wu")
            self.uid += 1
            A_t = st.enter_context(self.nc.sbuf_tensor("A_sb_%d" % self.uid, [128, 32, G], BF16))
            A = [Tile(A_t, Buf("A%d" % f)) for f in range(32)]
            gbufs = self.ring(st, 2, [128, G + 2], F32, "gbuf")
            cs = self.ring(st, 2, [128, G], F32, "cc")
            ges = self.ring(st, 2, [128, G], BF16, "ge")
            xts = self.ring(st, 2, [128, D], F32, "xtE")
            psG = self.ring(st, 2, [128, 1024], F32, "psG", psum=True)
            psU = self.ring(st, 2, [128, 512], F32, "psU", psum=True)
            psY = self.ps(st, [128, 512], F32, "psYE")
            it = 0
            pendE = [None]
            for gi in range(S // G):
                t0 = gi * G
                h2 = h2s[gi % 2]
                fw.load(h2, sq.H2T[:, :, t0:t0 + G + 2].rearrange("k p t -> p k t"), sq.H2T_b)
                for f in range(32):
                    wg, wu = wgs[it % 3], wus[it % 3]
                    pg, pu = psG[it % 2], psU[it % 2]
                    gbuf, c, ge = gbufs[it % 2], cs[it % 2], ges[it % 2]
                    it += 1
                    fw.load(wg, self.WG[l, f], self.WG_b)
                    fw.load(wu, self.WU[l, f], self.WU_b)
                    for k in range(8):
                        fw.mm(pg.t[:, 0:G], wg.t[:, k, :], h2.t[:, k, 1:G + 1], k == 0, k == 7, [wg, h2], [pg], False)
                    for k in range(8):
                        halo = bass.AP(h2.t[:].tensor, h2.t[:, k, 0:1].offset, [list(h2.t[:].ap[0]), [G + 1, 2]])
                        fw.mm(pg.t[:, G:G + 2], wg.t[:, k, :], halo, k == 0, k == 7, [wg, h2], [pg], k == 7)
                    for k in range(8):
                        fw.mm(pu.t[:], wu.t[:, k, :], h2.t[:, k, 1:G + 1], k == 0, k == 7, [wu, h2], [pu], k == 7)
                    fw.op("act", lambda e: e.copy(out=gbuf.t[:, 1:G + 1], in_=pg.t[:, 0:G]), [pg], [gbuf])
                    gh_out = bass.AP(gbuf.t[:].tensor, gbuf.t[:].offset, [list(gbuf.t[:].ap[0]), [G + 1, 2]])
                    fw.op("act", lambda e: e.copy(out=gh_out, in_=pg.t[:, G:G + 2]), [pg], [gbuf])
                    w0, w1, w2, bb = (cwb.t[:, j, f:f + 1] for j in range(4))
                    fw.op("dve", lambda e: e.tensor_scalar(out=c.t[:], in0=gbuf.t[:, 1:G + 1], scalar1=w1, scalar2=bb,
                                                           op0=ALU.mult, op1=ALU.add), [gbuf, cwb], [c])
                    fw.op("dve", lambda e: e.scalar_tensor_tensor(out=c.t[:], in0=gbuf.t[:, 0:G], scalar=w0, in1=c.t[:],
                                                                  op0=ALU.mult, op1=ALU.add), [gbuf, cwb, c], [c])
                    fw.op("dve", lambda e: e.scalar_tensor_tensor(out=c.t[:], in0=gbuf.t[:, 2:G + 2], scalar=w2, in1=c.t[:],
                                                                  op0=ALU.mult, op1=ALU.add), [gbuf, cwb, c], [c])
                    fw.op("act", lambda e: e.activation(out=ge.t[:], in_=c.t[:], func=AF.Gelu_apprx_tanh), [c], [ge])
                    fw.op("dve", lambda e: e.tensor_tensor(out=A_t[:, f, :], in0=ge.t[:], in1=pu.t[:], op=ALU.mult),
                          [ge, pu], [A[f]])
                    if f == 1 and pendE[0]:
                        pendE[0]()
                        pendE[0] = None
                for i in range(G // 128):
                    ts = slice(i * 128, (i + 1) * 128)
                    r0 = t0 + i * 128
                    xt = xts[i % 2]
                    ysb = ysbs[i % 2]
                    fw.load(xt, x_src[r0:r0 + 128, :], x_src_b)
                    for half in range(2):
                        hs = slice(half * 512, (half + 1) * 512)
                        for f in range(32):
                            fw.mm(psY.t[:], A_t[:, f, ts], wd.t[:, f, hs], f == 0, f == 31, [A[f], wd], [psY], f == 31)
                        fw.op("act", lambda e: e.copy(out=ysb.t[:, hs], in_=psY.t[:]), [psY], [ysb])
                    if pendE[0]:
                        pendE[0]()
                    pendE[0] = self.tail(ysb.t[:], ysb, xt, gpost, gnext, W, x_dst[r0:r0 + 128, :], x_dst_b,
                                         sq.HT_A[:, :, r0:r0 + 128], sq.HT_A_b, t32=ysb)
            if pendE[0]:
                pendE[0]()
        fw.barrier()

    def build(self):
        import os
        ph = os.environ.get("K_PHASES", "WI0ABCDEF")
        if "W" in ph:
            self.phase_W()
        for sq in self.seqs:
            if "I" in ph:
                self.phase_init(sq)
            if "0" in ph:
                self.phase_0(sq)
            for l in range(self.L):
                last = l == self.L - 1
                if "A" in ph:
                    self.phase_A(sq, l)
                if "B" in ph:
                    self.phase_B1(sq, l)
                if "C" in ph:
                    self.phase_B2(sq, l)
                if "D" in ph:
                    self.phase_C(sq, l)
                if "E" not in ph:
                    continue
                if l == 0:
                    xs, xsb = sq.x_in, self.in_buf
                else:
                    xs, xsb = sq.XB, sq.XB_b
                self.phase_D(sq, l, xs, xsb, sq.XA, sq.XA_b)
                if "F" not in ph:
                    continue
                if last:
                    self.phase_E(sq, l, sq.XA, sq.XA_b, sq.y_out, sq.y_b, True)
                else:
                    self.phase_E(sq, l, sq.XA, sq.XA_b, sq.XB, sq.XB_b, False)
        self.fw.barrier()


def _rope_tables(S):
    t = np.arange(S)
    inv = np.power(np.float32(500000.0), -np.arange(8, dtype=np.float32) / np.float32(8))
    ang = t.astype(np.float32)[:, None] * inv[None, :]
    c, s = np.cos(ang.astype(np.float64)), np.sin(ang.astype(np.float64))
    ropeA = np.concatenate([c, c, -s, s], axis=1).astype(np.float32)
    row = (t // 64).astype(np.float32)
    col = (t % 64).astype(np.float32)
    inv2 = np.power(np.float32(10000.0), -np.arange(16, dtype=np.float32) / np.float32(16))
    ar = (row[:, None] * inv2[None, :]).astype(np.float64)
    ac = (col[:, None] * inv2[None, :]).astype(np.float64)
    cr, sr, cc, sc = np.cos(ar), np.sin(ar), np.cos(ac), np.sin(ac)
    ropeB = np.concatenate([cr, cr, cc, cc, -sr, sr, -sc, sc], axis=1).astype(np.float32)
    return ropeA, ropeB


def _fft_consts(S):
    J = S // 128
    ncols = 128 // J
    m = np.arange(128)
    j = m % J
    cc = m // J
    fb = np.arange(128)
    ang = 2 * np.pi * (j[:, None] * fb[None, :]) / S
    tw = np.concatenate([np.cos(ang), np.sin(ang)], axis=1).astype(np.float32)
    fa = np.arange(J)
    n = np.arange(128)
    ncc, nfa = n // J, n % J
    same = (cc[:, None] == ncc[None, :])
    a2 = 2 * np.pi * (j[:, None] * nfa[None, :]) / J
    bdc = np.where(same, np.cos(a2), 0.0)
    bds = np.where(same, np.sin(a2), 0.0)
    bdcs = np.concatenate([bdc, bds], axis=1).astype(NPBF)
    return tw, bdcs


def _consts():
    c = {}
    c["c_identf"] = np.eye(128, dtype=np.float32)
    c["c_identb"] = np.eye(128, dtype=np.float32).astype(NPBF)
    k = np.arange(64)
    a = 2 * np.pi * (k[:, None] * k[None, :]) / 64
    bd = np.zeros((256, 512), np.float64)
    for g in range(4):
        bd[g * 64:(g + 1) * 64, g * 64:(g + 1) * 64] = np.cos(a)
        bd[g * 64:(g + 1) * 64, 256 + g * 64:256 + (g + 1) * 64] = np.sin(a)
    c["c_bd64"] = bd.astype(np.float32)
    i = np.arange(128)
    m = np.zeros((128, 2, 128), np.float32)
    m[:, 0, :] = (i[:, None] >= i[None, :])
    m[:, 1, :] = (i[:, None] <= i[None, :])
    c["c_masks"] = m.astype(NPBF)
    a = 2 * np.pi * (i[:, None] * i[None, :]) / 128
    C, Sn = np.cos(a), np.sin(a)
    c["c_rhsA"] = np.concatenate([C, -Sn], axis=1).astype(NPBF)
    c["c_rhsB"] = np.concatenate([-Sn, -C], axis=1).astype(NPBF)
    return c


def prep_weights(L, g_mix_pre, g_mix_post, w_in, g_q, g_k, g_heads, w_out, g_ffn_pre, g_ffn_post,
                 w_gate, w_up, conv_w, conv_b, w_down):
    f = lambda a: np.ascontiguousarray(np.asarray(a, dtype=np.float32))
    w_in = f(w_in)
    qa, ka, va = w_in[:, :, 0:256], w_in[:, :, 256:512], w_in[:, :, 512:768]
    qb, kb, vb, uc = w_in[:, :, 768:1280], w_in[:, :, 1280:1408], w_in[:, :, 1408:1536], w_in[:, :, 1536:1792]
    qbp = qb.reshape(L, D, 2, 4, 64).transpose(0, 1, 3, 2, 4).reshape(L, D, 512)
    m = {}
    m["w_in_r"] = np.ascontiguousarray(np.concatenate([qa, ka, qbp, kb, va, vb], axis=2))
    m["w_ucT"] = np.ascontiguousarray(uc.transpose(0, 2, 1))
    m["w_out"] = f(w_out)
    m["w_gate"] = f(w_gate)
    m["w_up"] = f(w_up)
    m["w_down"] = f(w_down)
    m["g_mix_pre"] = f(g_mix_pre)
    m["g_mix_post"] = f(g_mix_post)
    m["g_ffn_pre"] = f(g_ffn_pre)
    m["g_ffn_post"] = f(g_ffn_post)
    m["g_heads"] = f(g_heads)
    m["gqk"] = np.ascontiguousarray(np.stack([f(g_q), f(g_k)], axis=1))
    cw = f(conv_w).reshape(L, 3, 32, 128)
    cb = f(conv_b).reshape(L, 1, 32, 128)
    m["cwb"] = np.ascontiguousarray(np.concatenate([cw, cb], axis=1).transpose(0, 3, 1, 2))
    return m


_CACHE = {}


def get_builder(seq_lens, depth):
    key = (tuple(seq_lens), depth)
    if key not in _CACHE:
        _CACHE[key] = Builder(list(seq_lens), depth)
    return _CACHE[key]


def seq_consts(seq_lens):
    m = {}
    for S in set(seq_lens):
        ra, rb = _rope_tables(S)
        tw, bdcs = _fft_consts(S)
        m["ropeA_%d" % S] = ra
        m["ropeB_%d" % S] = rb
        m["tw_%d" % S] = tw
        m["bdcs_%d" % S] = bdcs
    return m


def kernel(x_prompt, x_sample, g_mix_pre, g_mix_post, w_in, g_q, g_k, g_heads, w_out,
           g_ffn_pre, g_ffn_post, w_gate, w_up, conv_w, conv_b, w_down):
    x_prompt = np.asarray(x_prompt, dtype=np.float32)
    x_sample = np.asarray(x_sample, dtype=np.float32)
    L = int(np.asarray(w_in).shape[0])
    SP, SS = x_prompt.shape[1], x_sample.shape[1]
    nB, nS = x_prompt.shape[0], x_sample.shape[0]
    b = get_builder((SP, SS), L)
    base = prep_weights(L, g_mix_pre, g_mix_post, w_in, g_q, g_k, g_heads, w_out, g_ffn_pre, g_ffn_post,
                        w_gate, w_up, conv_w, conv_b, w_down)
    base.update(_consts())
    base.update(seq_consts((SP, SS)))
    in_maps = []
    for c in range(8):
        m = dict(base)
        m["x0"] = np.ascontiguousarray(x_prompt[c % nB])
        m["x1"] = np.ascontiguousarray(x_sample[c % nS])
        in_maps.append(m)
    res = run_bass_kernel_spmd(b.nc, in_maps, core_ids=list(range(8)))
    yp = np.stack([np.asarray(res.results[c]["y0"], dtype=np.float32) for c in range(nB)], axis=0)
    ys = np.stack([np.asarray(res.results[c]["y1"], dtype=np.float32) for c in range(nS)], axis=0)
    return (yp, ys)
```

```python
import numpy as np
import ml_dtypes
from contextlib import ExitStack
import concourse.bass as bass
import concourse.mybir as mybir
from concourse.bass_utils import run_bass_kernel_spmd

F32 = mybir.dt.float32
BF16 = mybir.dt.bfloat16
AF = mybir.ActivationFunctionType
ALU = mybir.AluOpType
AX = mybir.AxisListType
NPBF = ml_dtypes.bfloat16

D = 1024
DFF = 4096
NCOL = 2048
EPS = 1e-6
PADR = 1024


class Sem:
    def __init__(self, nc, name):
        self.h = nc.semaphore(name).__enter__()
        self.count = 0
        self.name = name


class Buf:
    __slots__ = ("w", "r", "ro", "multi", "name", "excl")

    def __init__(self, name="", multi=False):
        self.excl = False
        self.w = {}
        self.r = {}
        self.ro = {}
        self.multi = multi
        self.name = name


class Tile:
    __slots__ = ("t", "b", "lsem", "ssem")

    def __init__(self, t, b):
        self.t = t
        self.b = b
        self.lsem = None
        self.ssem = None


def _b(x):
    return x.b if isinstance(x, Tile) else x


def _merge(dst, src):
    for s, v in src.items():
        if dst.get(s, 0) < v:
            dst[s] = v


class Q:
    def __init__(self, name, eng, sem):
        self.name = name
        self.eng = eng
        self.sem = sem
        self.waited = {}


class FW:
    def __init__(self, nc):
        self.nc = nc
        self.q = {}
        self.sems = []
        for name, eng in (("pe", nc.tensor), ("act", nc.scalar), ("dve", nc.vector),
                          ("pool", nc.gpsimd), ("sp", nc.sync)):
            s = Sem(nc, "q_" + name)
            self.sems.append(s)
            self.q[name] = Q(name, eng, s)
        self.dma_pool = {"l": [], "s": []}
        self.dma_idx = {"l": 0, "s": 0}
        self._rec = None

    def capture(self, fn):
        assert self._rec is None
        self._rec = []
        try:
            ret = fn()
        finally:
            rec, self._rec = self._rec, None
        return rec, ret

    @staticmethod
    def interleave(*lists):
        n = max(len(x) for x in lists)
        for i in range(n):
            for x in lists:
                if i < len(x):
                    x[i]()

    def dma_sem(self, kind="l"):
        pool = self.dma_pool[kind]
        if self.dma_idx[kind] >= len(pool):
            s = Sem(self.nc, "d%s%d" % (kind, len(pool)))
            pool.append(s)
            self.sems.append(s)
        s = pool[self.dma_idx[kind]]
        self.dma_idx[kind] += 1
        return s

    def _wait(self, q, reads, writes):
        deps = {}
        for b in reads:
            b = _b(b)
            _merge(deps, b.w)
            if b.excl:
                for s_, v_ in b.r.items():
                    if s_ is not q.sem and deps.get(s_, 0) < v_:
                        deps[s_] = v_
        for b in writes:
            b = _b(b)
            _merge(deps, b.r)
            _merge(deps, b.ro)
            if not b.multi:
                _merge(deps, b.w)
        for s, v in deps.items():
            if s is q.sem and q.name == "pe":
                continue
            if q.waited.get(s, 0) >= v:
                continue
            q.eng.wait_ge(s.h, v)
            q.waited[s] = v

    def _record(self, s, v, reads, writes):
        for b in reads:
            b = _b(b)
            if b.r.get(s, 0) < v:
                b.r[s] = v
        for b in writes:
            b = _b(b)
            if b.multi:
                if b.r:
                    _merge(b.ro, b.r)
                    b.r = {}
                    b.w = {}
                if b.w.get(s, 0) < v:
                    b.w[s] = v
            else:
                b.w = {s: v}
                b.r = {}

    def op(self, qn, emit, reads=(), writes=(), inc=True):
        if self._rec is not None:
            self._rec.append(lambda: self._op(qn, emit, reads, writes, inc))
            return
        self._op(qn, emit, reads, writes, inc)

    def _op(self, qn, emit, reads=(), writes=(), inc=True):
        q = self.q[qn]
        self._wait(q, reads, writes)
        ins = emit(q.eng)
        if inc:
            q.sem.count += 1
            ins.then_inc(q.sem.h, 1)
            v = q.sem.count
        else:
            v = q.sem.count + 1
        self._record(q.sem, v, reads, writes)

    def mm(self, out, lhsT, rhs, start, stop, reads, writes, last):
        self.op("pe", lambda e: e.matmul(out, lhsT=lhsT, rhs=rhs, start=start, stop=stop),
                reads, writes, inc=last)

    def tr(self, out, in_, ident, reads, writes, last):
        self.op("pe", lambda e: e.transpose(out=out, in_=in_, identity=ident), reads, writes, inc=last)

    def dma(self, qn, out, in_, reads, writes, sem):
        if self._rec is not None:
            self._rec.append(lambda: self._dma(qn, out, in_, reads, writes, sem))
            return
        self._dma(qn, out, in_, reads, writes, sem)

    def _dma(self, qn, out, in_, reads, writes, sem):
        q = self.q[qn]
        self._wait(q, reads, writes)
        q.eng.dma_start(out=out, in_=in_).then_inc(sem.h, 16)
        sem.count += 16
        self._record(sem, sem.count, reads, writes)

    def load(self, tile, src, src_buf, out=None, qn="sp"):
        if tile.lsem is None:
            tile.lsem = self.dma_sem("l" if qn == "sp" else "s")
        self.dma(qn, tile.t[:] if out is None else out, src, [src_buf], [tile], tile.lsem)

    def store(self, dst, dst_buf, tile, src=None, qn="pool"):
        if tile.ssem is None:
            tile.ssem = self.dma_sem("l" if qn == "sp" else "s")
        self.dma(qn, dst, tile.t[:] if src is None else src, [tile], [dst_buf], tile.ssem)

    def barrier(self):
        for q in self.q.values():
            for s in self.sems:
                if s is q.sem:
                    continue
                if s.count > q.waited.get(s, 0):
                    q.eng.wait_ge(s.h, s.count)
                    q.waited[s] = s.count
        self.dma_idx = {"l": 0, "s": 0}


def bc_mid(ap, n):
    a = [list(x) for x in ap.ap]
    return bass.AP(ap.tensor, ap.offset, [a[0], [0, n]] + a[1:])


def bc_last(ap, n):
    a = [list(x) for x in ap.ap]
    return bass.AP(ap.tensor, ap.offset, a + [[0, n]])


def rows_ap(ap2d, row0, rstep, nrows, c0, ncols):
    rs = ap2d.ap[0][0]
    return bass.AP(ap2d.tensor, ap2d.offset + row0 * rs + c0, [[rstep * rs, nrows], [1, ncols]])


class Seq:
    pass


class Builder:
    def __init__(self, seq_lens, depth, taps=False):
        self.nc = nc = bass.Bass("TRN2", target_bir_lowering=False)
        self.fw = FW(nc)
        self.L = L = depth
        self.uid = 0
        self.taps = taps
        self.in_buf = Buf("inputs", multi=True)

        def din(name, shape, dt=F32):
            return nc.dram_tensor(name, list(shape), dt, kind="ExternalInput").ap()

        self.w_in_r = din("w_in_r", [L, D, 1536])
        self.w_ucT = din("w_ucT", [L, 256, D])
        self.w_out = din("w_out", [L, D, D])
        self.w_gate = din("w_gate", [L, D, DFF])
        self.w_up = din("w_up", [L, D, DFF])
        self.w_down = din("w_down", [L, DFF, D])
        self.g_mix_pre = din("g_mix_pre", [L, D])
        self.g_mix_post = din("g_mix_post", [L, D])
        self.g_ffn_pre = din("g_ffn_pre", [L, D])
        self.g_ffn_post = din("g_ffn_post", [L, D])
        self.g_heads = din("g_heads", [L, D])
        self.gqk = din("gqk", [L, 2, 64])
        self.cwb = din("cwb", [L, 128, 4, 32])
        self.c_identf = din("c_identf", [128, 128])
        self.c_identb = din("c_identb", [128, 128], BF16)
        self.c_bd64 = din("c_bd64", [256, 512])
        self.c_masks = din("c_masks", [128, 2, 128], BF16)
        self.c_rhsA = din("c_rhsA", [128, 256], BF16)
        self.c_rhsB = din("c_rhsB", [128, 256], BF16)

        def dsc(name, shape, dt):
            return nc.dram_tensor(name, list(shape), dt).ap(), Buf(name, multi=True)

        self.W_IN, self.W_IN_b = dsc("W_IN", [L, D, NCOL], BF16)
        self.W_OUT, self.W_OUT_b = dsc("W_OUT", [L, D, D], BF16)
        self.WG, self.WG_b = dsc("WG", [L, 32, 128, 8, 128], BF16)
        self.WU, self.WU_b = dsc("WU", [L, 32, 128, 8, 128], BF16)
        self.WD, self.WD_b = dsc("WD", [L, DFF, D], BF16)

        self.seqs = []
        seen = {}
        for si, S in enumerate(seq_lens):
            sq = Seq()
            sq.S = S
            sq.i = si
            sq.x_in = din("x%d" % si, [S, D])
            sq.y_out = nc.dram_tensor("y%d" % si, [S, D], F32, kind="ExternalOutput").ap()
            sq.y_b = Buf("y%d" % si, multi=True)
            if S not in seen:
                seen[S] = (din("ropeA_%d" % S, [S, 32]), din("ropeB_%d" % S, [S, 128]),
                           din("tw_%d" % S, [128, 256]), din("bdcs_%d" % S, [128, 256], BF16))
            sq.ropeA, sq.ropeB, sq.tw, sq.bdcs = seen[S]
            n = "s%d_" % si
            sq.HT_A, sq.HT_A_b = dsc(n + "HTA", [8, 128, S], BF16)
            sq.H2T, sq.H2T_b = dsc(n + "H2T", [8, 128, S + 2], BF16)
            sq.XA, sq.XA_b = dsc(n + "XA", [S, D], F32)
            sq.XB, sq.XB_b = dsc(n + "XB", [S, D], F32)
            sq.QA, sq.QA_b = dsc(n + "QA", [S, 256], BF16)
            sq.KA, sq.KA_b = dsc(n + "KA", [S + 2 * PADR, 256], BF16)
            sq.VA, sq.VA_b = dsc(n + "VA", [S + 2 * PADR, 260], BF16)
            sq.QTB, sq.QTB_b = dsc(n + "QTB", [4, 128, S], BF16)
            sq.KTB, sq.KTB_b = dsc(n + "KTB", [128, S], BF16)
            sq.VB, sq.VB_b = dsc(n + "VB", [S, 130], BF16)
            sq.AB, sq.AB_b = dsc(n + "AB", [S, 512], BF16)
            sq.ACC = []
            sq.ACC_b = []
            for d in (1, 4, 16):
                a, b = dsc(n + "ACC%d" % d, [S, 260], F32)
                sq.ACC.append(a)
                sq.ACC_b.append(b)
            sq.OMIX, sq.OMIX_b = dsc(n + "OMIX", [S, D], BF16)
            self.seqs.append(sq)

        self.build()

    def sb(self, st, shape, dt, name="t"):
        self.uid += 1
        t = st.enter_context(self.nc.sbuf_tensor("%s_%d" % (name, self.uid), list(shape), dt))
        return Tile(t, Buf(name))

    def ps(self, st, shape, dt=F32, name="p"):
        self.uid += 1
        nbytes = int(np.prod(shape[1:])) * (2 if dt == BF16 else 4)
        assert nbytes % 2048 == 0, (name, shape)
        t = st.enter_context(self.nc.psum_tensor("%s_%d" % (name, self.uid), list(shape), dt))
        tl = Tile(t, Buf(name))
        tl.b.excl = True
        return tl

    def ring(self, st, n, shape, dt, name="r", psum=False):
        return [(self.ps if psum else self.sb)(st, shape, dt, name) for _ in range(n)]

    def gain_tile(self, st, src_row):
        t = self.sb(st, [128, D], F32, "gain")
        self.fw.load(t, src_row.broadcast_to([128, D]), self.in_buf)
        return t

    def const_tile(self, st, src, shape, dt):
        t = self.sb(st, shape, dt, "const")
        self.fw.load(t, src, self.in_buf)
        return t

    def eps_tile(self, st):
        t = self.sb(st, [128, 1], F32, "eps")
        self.fw.op("dve", lambda e: e.memset(t.t[:], EPS), [], [t])
        return t

    def rstd(self, ss, out, eps, scale):
        fw = self.fw
        fw.op("act", lambda e: e.activation(out=out.t[:], in_=ss.t[:], func=AF.Ln, bias=eps.t[:, 0:1], scale=scale),
              [ss, eps], [out])
        fw.op("act", lambda e: e.activation(out=out.t[:], in_=out.t[:], func=AF.Exp, scale=-0.5), [out], [out])

    def norm_h_store(self, src_ap, src_t, gnext, W, dstT, dstT_b):
        fw = self.fw
        ss, rs, psT, eps, identb = W["ss"], W["rs"], W["psT"], W["eps"], W["identb"]
        h = W["hs"][W["hi"] % 2]
        hT = W["hTs"][W["hi"] % 2]
        W["hi"] += 1
        fw.op("act", lambda e: e.activation(out=h.t[:], in_=src_ap, func=AF.Square, scale=1.0 / 32.0,
                                            accum_out=ss.t[:, 0:1]), [src_t], [h, ss])
        self.rstd(ss, rs, eps, 1.0)
        fw.op("dve", lambda e: e.scalar_tensor_tensor(out=h.t[:], in0=src_ap, scalar=rs.t[:, 0:1], in1=gnext.t[:],
                                                      op0=ALU.mult, op1=ALU.mult), [src_t, rs, gnext], [h])

        def part2():
            for k in range(8):
                fw.tr(psT.t[:, k, :], h.t[:, k * 128:(k + 1) * 128], identb.t[:], [h, identb], [psT], k == 7)
            fw.op("act", lambda e: e.copy(out=hT.t[:], in_=psT.t[:]), [psT], [hT])
            fw.store(dstT.rearrange("k p t -> p k t"), dstT_b, hT)
        return part2

    def tail(self, y_ap, y_t, xt, gpost, gnext, W, dst_x, dst_x_b, dstT, dstT_b, t32=None):
        fw = self.fw
        ss, rs, eps = W["ss"], W["rs"], W["eps"]
        if t32 is None:
            t32 = W["t32"]
        junk = W["hs"][W["hi"] % 2]
        xn = W["xns"][W["xi"] % 2]
        W["xi"] += 1
        fw.op("act", lambda e: e.activation(out=junk.t[:], in_=y_ap, func=AF.Square, scale=1.0 / 32.0,
                                            accum_out=ss.t[:, 0:1]), [y_t], [junk, ss])
        self.rstd(ss, rs, eps, 1.0)
        fw.op("dve", lambda e: e.scalar_tensor_tensor(out=t32.t[:], in0=y_ap, scalar=rs.t[:, 0:1], in1=gpost.t[:],
                                                      op0=ALU.mult, op1=ALU.mult), [y_t, rs, gpost], [t32])
        fw.op("pool", lambda e: e.tensor_tensor(out=xn.t[:], in0=xt.t[:], in1=t32.t[:], op=ALU.add), [xt, t32], [xn])
        fw.store(dst_x, dst_x_b, xn)
        if gnext is not None:
            return self.norm_h_store(xn.t[:], xn, gnext, W, dstT, dstT_b)
        return None

    def tail_ws(self, st):
        W = {}
        W["ss"] = self.sb(st, [128, 1], F32, "ss")
        W["rs"] = self.sb(st, [128, 1], F32, "rs")
        W["hs"] = self.ring(st, 2, [128, D], BF16, "h")
        W["hTs"] = self.ring(st, 2, [128, 8, 128], BF16, "hT")
        W["hi"] = 0
        W["xi"] = 0
        W["psT"] = self.ps(st, [128, 8, 128], BF16, "psT")
        W["eps"] = self.eps_tile(st)
        W["identb"] = self.const_tile(st, self.c_identb, [128, 128], BF16)
        return W

    def w_items(self):
        L = self.L
        early, late = [], []
        for l in range(L):
            for rt in range(8):
                r = slice(rt * 128, (rt + 1) * 128)
                (early if l == 0 else late).append((self.w_in_r[l, r, :], self.W_IN[l, r, 0:1536], self.W_IN_b, 1536, None))
            for rt in range(8):
                r = slice(rt * 128, (rt + 1) * 128)
                late.append((self.w_out[l, r, :], self.W_OUT[l, r, :], self.W_OUT_b, 1024, None))
                for src, dst, db in ((self.w_gate, self.WG, self.WG_b), (self.w_up, self.WU, self.WU_b)):
                    for ch in range(2):
                        late.append((src[l, r, ch * 2048:(ch + 1) * 2048],
                                     dst[l, ch * 16:(ch + 1) * 16, :, rt, :].rearrange("f p m -> p f m"),
                                     db, 2048, 16))
            for rt in range(32):
                r = slice(rt * 128, (rt + 1) * 128)
                late.append((self.w_down[l, r, :], self.WD[l, r, :], self.WD_b, 1024, None))
        return early, late

    def emit_w_item(self, item, i, s32, s16, engs):
        fw = self.fw
        src, dst, db, C, nf = item
        a = s32[i % len(s32)]
        b = s16[i % len(s16)]
        fw.load(a, src, self.in_buf, out=a.t[:, 0:C])
        en = engs[i % len(engs)]
        if en == "act":
            fw.op("act", lambda e: e.copy(out=b.t[:, 0:C], in_=a.t[:, 0:C]), [a], [b])
        else:
            fw.op(en, lambda e: e.tensor_copy(out=b.t[:, 0:C], in_=a.t[:, 0:C]), [a], [b])
        if nf is None:
            fw.store(dst, db, b, src=b.t[:, 0:C])
        else:
            fw.store(dst, db, b, src=b.t[:, 0:C].rearrange("p (f m) -> p f m", f=nf))

    def phase_W(self):
        fw, nc, L = self.fw, self.nc, self.L
        with ExitStack() as st:
            s32 = self.ring(st, 3, [128, 2048], F32, "s32")
            s16 = self.ring(st, 3, [128, 2048], BF16, "s16")
            early, late = self.w_items()
            self.w_late = late
            for i, it in enumerate(early):
                self.emit_w_item(it, i, s32, s16, ["dve", "act", "pool"])
            bd = self.sb(st, [128, 2, 512], F32, "bd")
            fw.load(bd, self.c_bd64.rearrange("(k p) n -> p k n", p=128), self.in_buf)
            wu = self.ring(st, 2, [128, 2, D], F32, "wucT")
            pf = self.ring(st, 2, [128, 512], F32, "pf", psum=True)
            fo = self.ring(st, 2, [128, 512], BF16, "fo")
            for l in range(L):
                w = wu[l % 2]
                fw.load(w, self.w_ucT[l].rearrange("(k p) n -> p k n", p=128), self.in_buf)
                for dc in range(8):
                    p = pf[dc % 2]
                    o = fo[dc % 2]
                    for kc in range(2):
                        fw.mm(p.t[:], w.t[:, kc, dc * 128:(dc + 1) * 128], bd.t[:, kc, :], kc == 0, kc == 1,
                              [w, bd], [p], kc == 1)
                    fw.op("dve", lambda e: e.tensor_copy(out=o.t[:], in_=p.t[:]), [p], [o])
                    fw.store(self.W_IN[l, dc * 128:(dc + 1) * 128, 1536:2048], self.W_IN_b, o)
        fw.barrier()

    def phase_init(self, sq):
        fw = self.fw
        S = sq.S
        with ExitStack() as st:
            z = self.sb(st, [128, 8, 260], BF16, "zero")
            fw.op("dve", lambda e: e.memset(z.t[:], 0.0), [], [z])
            for base in (0, PADR + S):
                fw.store(sq.KA[base:base + PADR, :].rearrange("(i p) c -> p i c", p=128), sq.KA_b, z,
                         src=z.t[:, :, 0:256])
                fw.store(sq.VA[base:base + PADR, :].rearrange("(i p) c -> p i c", p=128), sq.VA_b, z)
            with self.nc.allow_non_contiguous_dma(reason="halo zero columns"):
                for c0 in (0, S + 1):
                    fw.store(sq.H2T[:, :, c0:c0 + 1].rearrange("k p t -> p k t"), sq.H2T_b, z, src=z.t[:, :, 0:1])
        fw.barrier()

    def phase_0(self, sq):
        fw = self.fw
        with ExitStack() as st:
            W = self.tail_ws(st)
            g = self.gain_tile(st, self.g_mix_pre[0:1, :])
            xs = self.ring(st, 3, [128, D], F32, "x0")
            pend = None
            for ti in range(sq.S // 128):
                xt = xs[ti % 3]
                fw.load(xt, sq.x_in[ti * 128:(ti + 1) * 128, :], self.in_buf)
                p2 = self.norm_h_store(xt.t[:], xt, g, W, sq.HT_A[:, :, ti * 128:(ti + 1) * 128], sq.HT_A_b)
                if pend:
                    pend()
                pend = p2
            pend()
        fw.barrier()

    def phase_A(self, sq, l):
        fw = self.fw
        S = sq.S
        with ExitStack() as st:
            w = self.sb(st, [128, 8, NCOL], BF16, "wA")
            w.b.multi = True
            for k in range(8):
                if w.lsem is None:
                    w.lsem = fw.dma_sem()
                fw.dma("sp", w.t[:, k, :], self.W_IN[l, k * 128:(k + 1) * 128, :], [self.W_IN_b], [w], w.lsem)
            gqk = self.sb(st, [128, 10, 64], F32, "gqk")
            gqk.b.multi = True
            gqk.lsem = fw.dma_sem()
            fw.dma("sp", gqk.t[:, 0:8, :], self.gqk[l:l + 1, 0:1, :].broadcast_to([128, 8, 64]), [self.in_buf], [gqk], gqk.lsem)
            fw.dma("sp", gqk.t[:, 8:10, :], self.gqk[l:l + 1, 1:2, :].broadcast_to([128, 2, 64]), [self.in_buf], [gqk], gqk.lsem)
            eps = self.eps_tile(st)
            identb = self.const_tile(st, self.c_identb, [128, 128], BF16)
            hTs = self.ring(st, 5, [128, 8, 128], BF16, "hTa")
            rAs = self.ring(st, 5, [128, 32], F32, "rA")
            rBs = self.ring(st, 5, [128, 128], F32, "rB")
            banks = self.ring(st, 6, [128, 512], F32, "pA", psum=True)
            psT = self.ps(st, [128, 8, 128], BF16, "psTA")
            qkas = self.ring(st, 2, [128, 512], BF16, "qka")
            tmpa = self.sb(st, [128, 8, 16], F32, "tmpa")
            tmpb = self.sb(st, [128, 8, 16], F32, "tmpb")
            def nr_scratch(nh):
                return (self.sb(st, [128, nh, 64], F32, "sqj"), self.sb(st, [128, nh], F32, "ssb"),
                        self.sb(st, [128, nh], F32, "rsb"), self.sb(st, [128, nh, 64], F32, "tn"),
                        self.sb(st, [128, nh, 64], F32, "ra"), self.sb(st, [128, nh, 64], F32, "rb"))
            scq = nr_scratch(8)
            sck = nr_scratch(2)
            qqs = self.ring(st, 2, [128, 512], BF16, "qq")
            kks = self.ring(st, 2, [128, 128], BF16, "kk")
            qkT = self.ring(st, 2, [128, 5, 128], BF16, "qkT")
            vas = self.ring(st, 2, [128, 4, 65], BF16, "va")
            vbs = self.ring(st, 2, [128, 2, 65], BF16, "vb")
            abs_ = self.ring(st, 2, [128, 512], BF16, "ab")
            for v in vas + vbs:
                fw.op("pool", lambda e: e.memset(v.t[:], 1.0), [], [v])
            pendA = None

            def loadsA(tj):
                tt = tj * 128
                fw.load(hTs[tj % 5], sq.HT_A[:, :, tt:tt + 128].rearrange("k p t -> p k t"), sq.HT_A_b)
                fw.load(rAs[tj % 5], sq.ropeA[tt:tt + 128, :], self.in_buf)
                fw.load(rBs[tj % 5], sq.ropeB[tt:tt + 128, :], self.in_buf)

            for tj in range(min(3, S // 128)):
                loadsA(tj)
            for ti in range(S // 128):
                t0 = ti * 128
                hT = hTs[ti % 5]
                rA = rAs[ti % 5]
                rB = rBs[ti % 5]
                if ti + 3 < S // 128:
                    loadsA(ti + 3)
                bk = [banks[(ti * 4 + nb) % 6] for nb in range(4)]
                for nb in range(4):
                    for k in range(8):
                        fw.mm(bk[nb].t[:], hT.t[:, k, :], w.t[:, k, nb * 512:(nb + 1) * 512], k == 0, k == 7,
                              [hT, w], [bk[nb]], k == 7)
                p0, p1, p2, p3 = bk
                qka = qkas[ti % 2]
                qq, kk = qqs[ti % 2], kks[ti % 2]
                va, vb, ab = vas[ti % 2], vbs[ti % 2], abs_[ti % 2]

                def chain0(p0=p0, qka=qka, rA=rA, t0=t0):
                    fw.op("act", lambda e: e.copy(out=qka.t[:], in_=p0.t[:]), [p0], [qka])
                    pv = p0.t[:].rearrange("p (h d) -> p h d", d=64)
                    ov = qka.t[:].rearrange("p (h d) -> p h d", d=64)
                    fw.op("dve", lambda e: e.tensor_tensor(out=tmpa.t[:], in0=pv[:, :, 0:16], in1=bc_mid(rA.t[:, 0:16], 8),
                                                           op=ALU.mult), [p0, rA], [tmpa])
                    fw.op("dve", lambda e: e.tensor_tensor(out=tmpb.t[:, :, 0:8], in0=pv[:, :, 8:16],
                                                           in1=bc_mid(rA.t[:, 16:24], 8), op=ALU.mult), [p0, rA], [tmpb])
                    fw.op("dve", lambda e: e.tensor_tensor(out=tmpb.t[:, :, 8:16], in0=pv[:, :, 0:8],
                                                           in1=bc_mid(rA.t[:, 24:32], 8), op=ALU.mult), [p0, rA], [tmpb])
                    fw.op("dve", lambda e: e.tensor_tensor(out=ov[:, :, 0:16], in0=tmpa.t[:], in1=tmpb.t[:], op=ALU.add),
                          [tmpa, tmpb], [qka])
                    fw.store(sq.QA[t0:t0 + 128, :], sq.QA_b, qka, src=qka.t[:, 0:256], qn="sp")
                    fw.store(sq.KA[PADR + t0:PADR + t0 + 128, :], sq.KA_b, qka, src=qka.t[:, 256:512], qn="sp")

                def chain_nr(srct, src, h0, nh, sc, dst, rB=rB):
                    sqj, ssb, rsb, tn, ra, rb = sc
                    gs = slice(h0, h0 + nh)
                    sv = src.rearrange("p (h d) -> p h d", d=64)
                    fw.op("act", lambda e: e.activation(out=sqj.t[:], in_=sv, func=AF.Square), [srct], [sqj])
                    fw.op("dve", lambda e: e.tensor_reduce(out=ssb.t[:], in_=sqj.t[:], axis=AX.X, op=ALU.add), [sqj], [ssb])
                    fw.op("act", lambda e: e.activation(out=rsb.t[:], in_=ssb.t[:], func=AF.Ln,
                                                        bias=eps.t[:, 0:1], scale=1.0 / 64.0), [ssb, eps], [rsb])
                    fw.op("act", lambda e: e.activation(out=rsb.t[:], in_=rsb.t[:], func=AF.Exp, scale=-0.5), [rsb], [rsb])
                    fw.op("dve", lambda e: e.tensor_tensor(out=tn.t[:], in0=sv, in1=bc_last(rsb.t[:], 64), op=ALU.mult),
                          [srct, rsb], [tn])
                    fw.op("pool", lambda e: e.tensor_tensor(out=tn.t[:], in0=tn.t[:], in1=gqk.t[:, gs, :], op=ALU.mult),
                          [tn, gqk], [tn])
                    fw.op("dve", lambda e: e.tensor_tensor(out=ra.t[:], in0=tn.t[:], in1=bc_mid(rB.t[:, 0:64], nh),
                                                           op=ALU.mult), [tn, rB], [ra])
                    t5 = tn.t[:].rearrange("p h (b x d) -> p h b x d", b=2, x=2)
                    r5 = rb.t[:].rearrange("p h (b x d) -> p h b x d", b=2, x=2)
                    s4 = rB.t[:, 64:128].rearrange("p (b x d) -> p b x d", b=2, x=2)
                    fw.op("pool", lambda e: e.tensor_tensor(out=r5[:, :, :, 0, :], in0=t5[:, :, :, 1, :],
                                                            in1=bc_mid(s4[:, :, 0, :], nh), op=ALU.mult), [tn, rB], [rb])
                    fw.op("pool", lambda e: e.tensor_tensor(out=r5[:, :, :, 1, :], in0=t5[:, :, :, 0, :],
                                                            in1=bc_mid(s4[:, :, 1, :], nh), op=ALU.mult), [tn, rB], [rb])
                    fw.op("dve", lambda e: e.tensor_tensor(out=dst.t[:].rearrange("p (h d) -> p h d", d=64),
                                                           in0=ra.t[:], in1=rb.t[:], op=ALU.add), [ra, rb], [dst])

                def chainv(p2=p2, p3=p3, va=va, vb=vb, ab=ab, t0=t0):
                    fw.op("act", lambda e: e.copy(out=va.t[:, :, 0:64], in_=p2.t[:, 128:384].rearrange("p (h d) -> p h d", d=64)),
                          [p2], [va])
                    fw.op("act", lambda e: e.copy(out=vb.t[:, :, 0:64], in_=p2.t[:, 384:512].rearrange("p (h d) -> p h d", d=64)),
                          [p2], [vb])
                    fw.store(sq.VA[PADR + t0:PADR + t0 + 128, :], sq.VA_b, va, src=va.t[:].rearrange("p h d -> p (h d)"), qn="sp")
                    fw.store(sq.VB[t0:t0 + 128, :], sq.VB_b, vb, src=vb.t[:].rearrange("p h d -> p (h d)"), qn="sp")
                    fw.op("dve", lambda e: e.tensor_copy(out=ab.t[:], in_=p3.t[:]), [p3], [ab])
                    fw.store(sq.AB[t0:t0 + 128, :], sq.AB_b, ab, qn="sp")

                c0, _ = fw.capture(chain0)
                cq, _ = fw.capture(lambda: chain_nr(p1, p1.t[:, :], 0, 8, scq, qq))
                ck, _ = fw.capture(lambda: chain_nr(p2, p2.t[:, 0:128], 8, 2, sck, kk))
                cv, _ = fw.capture(chainv)
                fw.interleave(cq, c0, ck, cv)
                if pendA:
                    pendA()

                def mk_tr(qq=qq, kk=kk, ti=ti, t0=t0):
                    def f():
                        for c in range(4):
                            fw.tr(psT.t[:, c, :], qq.t[:, c * 128:(c + 1) * 128], identb.t[:], [qq, identb], [psT], False)
                        fw.tr(psT.t[:, 4, :], kk.t[:], identb.t[:], [kk, identb], [psT], True)
                        qT = qkT[ti % 2]
                        fw.op("act", lambda e: e.copy(out=qT.t[:], in_=psT.t[:, 0:5, :]), [psT], [qT])
                        fw.store(sq.QTB[:, :, t0:t0 + 128].rearrange("r p t -> p r t"), sq.QTB_b, qT, src=qT.t[:, 0:4, :], qn="sp")
                        fw.store(sq.KTB[:, t0:t0 + 128], sq.KTB_b, qT, src=qT.t[:, 4, :], qn="sp")
                    return f
                pendA = mk_tr()
            pendA()
        fw.barrier()

    def phase_B1(self, sq, l):
        fw = self.fw
        S = sq.S
        with ExitStack() as st:
            masks = self.const_tile(st, self.c_masks, [128, 2, 128], BF16)
            identb = self.const_tile(st, self.c_identb, [128, 128], BF16)
            Qts = self.ring(st, 5, [128, 256], BF16, "Qt")
            Kts = self.ring(st, 6, [128, 256], BF16, "Kt")
            Vts = self.ring(st, 10, [128, 260], BF16, "Vt")
            QTs = self.ring(st, 3, [128, 2, 128], BF16, "QT")
            KTs = self.ring(st, 8, [128, 2, 128], BF16, "KT")
            Pes = self.ring(st, 3, [128, 2, 256], BF16, "Pe")
            Pms = self.ring(st, 6, [128, 2, 2, 128], BF16, "Pm")
            Osbs = self.ring(st, 3, [128, 260], F32, "Osb")
            psTs = self.ring(st, 2, [128, 8, 128], BF16, "psTb", psum=True)
            psSs = self.ring(st, 2, [128, 2, 512], F32, "psS", psum=True)
            psOs = self.ring(st, 2, [128, 512], F32, "psO", psum=True)
            cnt = {"k": 0, "q": 0, "s": 0, "t": 0, "o": 0}
            blocks = []
            for di, d in enumerate((1, 4, 16)):
                nb = (S // d) // 128
                for r in range(d):
                    for b in range(nb):
                        blocks.append((di, d, r, b))
            state = {}
            ktiles = {}

            kl = {}
            ql = {}

            def load_k0(d, r, j):
                i = cnt["k"]
                cnt["k"] += 1
                Kt, Vt = Kts[i % 6], Vts[i % 10]
                row0 = PADR + d * (128 * j - 64) + r
                fw.load(Kt, rows_ap(sq.KA, row0, d, 128, 0, 256), sq.KA_b)
                fw.load(Vt, rows_ap(sq.VA, row0, d, 128, 0, 260), sq.VA_b)
                kl[(d, r, j)] = (Kt, Vt, KTs[i % 8])

            def S0(bi):
                di, d, r, b = blocks[bi]
                if b == 0:
                    load_k0(d, r, 0)
                load_k0(d, r, b + 1)
                qi = cnt["q"]
                cnt["q"] += 1
                Qt = Qts[qi % 5]
                qrow0 = d * 128 * b + r
                fw.load(Qt, rows_ap(sq.QA, qrow0, d, 128, 0, 256), sq.QA_b)
                ql[bi] = (Qt, QTs[qi % 3], qrow0)

            def tr_k(d, r, j):
                Kt, Vt, KT = kl.pop((d, r, j))
                pT = psTs[cnt["t"] % 2]
                cnt["t"] += 1
                for c in range(2):
                    fw.tr(pT.t[:, c, :], Kt.t[:, c * 128:(c + 1) * 128], identb.t[:], [Kt, identb], [pT], c == 1)
                fw.op("dve", lambda e: e.tensor_copy(out=KT.t[:], in_=pT.t[:, 0:2, :]), [pT], [KT])
                ktiles[(d, r, j)] = (KT, Vt)

            def S1(bi):
                di, d, r, b = blocks[bi]
                if b == 0:
                    tr_k(d, r, 0)
                tr_k(d, r, b + 1)
                Qt, QT, qrow0 = ql.pop(bi)
                pT = psTs[cnt["t"] % 2]
                cnt["t"] += 1
                for c in range(2):
                    fw.tr(pT.t[:, c, :], Qt.t[:, c * 128:(c + 1) * 128], identb.t[:], [Qt, identb], [pT], c == 1)
                fw.op("dve", lambda e: e.tensor_copy(out=QT.t[:], in_=pT.t[:, 0:2, :]), [pT], [QT])
                state[bi] = {"QT": QT, "k": [ktiles[(d, r, b)], ktiles[(d, r, b + 1)]], "qrow0": qrow0}
                del ktiles[(d, r, b)]

            def S2(bi):
                stt = state[bi]
                QT = stt["QT"]
                pms = []
                for jj in range(2):
                    KT, Vt = stt["k"][jj]
                    si = cnt["s"]
                    cnt["s"] += 1
                    pS, Pe, Pm = psSs[si % 2], Pes[si % 3], Pms[si % 6]
                    for h in range(4):
                        c, hp = h // 2, h % 2
                        po = hp * 64
                        fw.mm(pS.t[:, hp, c * 128:(c + 1) * 128], KT.t[po:po + 64, c, :], QT.t[po:po + 64, c, :],
                              True, True, [KT, QT], [pS], h == 3)
                    fw.op("act", lambda e: e.activation(out=Pe.t[:], in_=pS.t[:, :, 0:256], func=AF.Exp, scale=0.125),
                          [pS], [Pe])
                    mk = bass.AP(masks.t[:].tensor, masks.t[:, jj, :].offset,
                                 [list(masks.t[:].ap[0]), [0, 2], [0, 2], [1, 128]])
                    fw.op("dve",
                          lambda e: e.tensor_tensor(out=Pm.t[:], in0=Pe.t[:].rearrange("p a (c q) -> p a c q", c=2),
                                                    in1=mk, op=ALU.mult), [Pe, masks], [Pm])
                    pms.append((Pm, Vt))
                stt["pms"] = pms

            def S3(bi):
                di, d, r, b = blocks[bi]
                stt = state.pop(bi)
                oi = cnt["o"]
                cnt["o"] += 1
                pO = psOs[oi % 2]
                for h in range(4):
                    c, hp = h // 2, h % 2
                    for jj in range(2):
                        Pm, Vt = stt["pms"][jj]
                        fw.mm(pO.t[:, h * 65:(h + 1) * 65], Pm.t[:, hp, c, :], Vt.t[:, h * 65:(h + 1) * 65],
                              jj == 0, jj == 1, [Pm, Vt], [pO], h == 3 and jj == 1)
                Osb = Osbs[oi % 3]
                fw.op("act", lambda e: e.copy(out=Osb.t[:], in_=pO.t[:, 0:260]), [pO], [Osb])
                fw.store(rows_ap(sq.ACC[di], stt["qrow0"], d, 128, 0, 260), sq.ACC_b[di], Osb)

            n = len(blocks)
            S0(0)
            if n > 1:
                S0(1)
            for step in range(n + 2):
                if step + 2 < n:
                    S0(step + 2)
                if step < n:
                    S1(step)
                if 1 <= step <= n:
                    S2(step - 1)
                if step >= 2:
                    S3(step - 2)
        fw.barrier()

    def phase_B2(self, sq, l):
        fw = self.fw
        S = sq.S
        nkb = S // 128
        nqg = S // 512
        with ExitStack() as st:
            KT = self.sb(st, [128, S], BF16, "KTb")
            fw.load(KT, sq.KTB, sq.KTB_b)
            VB = self.sb(st, [128, nkb, 130], BF16, "VBb")
            fw.load(VB, sq.VB.rearrange("(k p) c -> p k c", p=128), sq.VB_b)
            identf = self.const_tile(st, self.c_identf, [128, 128], F32)
            Qgs = self.ring(st, 3, [128, 512], BF16, "Qg")
            Pts = self.ring(st, 4, [128, 2, 512], BF16, "Pt")
            Osbs = self.ring(st, 2, [128, 2, 512], F32, "OsbB")
            rdens = self.ring(st, 2, [128, 4, 1], F32, "rden")
            obs = self.ring(st, 3, [128, 4, 64], BF16, "ob")
            psSs = self.ring(st, 3, [128, 2, 512], F32, "psSB", psum=True)
            psO = self.ring(st, 2, [128, 512], F32, "psOB", psum=True)
            bg = list(getattr(self, "w_late", []))
            self.w_late = []
            if bg:
                bg32 = self.ring(st, 3, [128, 2048], F32, "bg32")
                bg16 = self.ring(st, 3, [128, 2048], BF16, "bg16")
            bgi = [0]

            def bg_step():
                if bgi[0] < len(bg):
                    self.emit_w_item(bg[bgi[0]], bgi[0], bg32, bg16, ["dve"])
                    bgi[0] += 1
            groups = [(r, qg) for r in range(4) for qg in range(nqg)]
            iters = [(gi, kb) for gi in range(len(groups)) for kb in range(nkb)]
            cnt = {"u": 0, "ps": 0}

            def load_q(gi):
                r, qg = groups[gi]
                fw.load(Qgs[gi % 3], sq.QTB[r, :, qg * 512:(qg + 1) * 512], sq.QTB_b)

            def emit_qk(i):
                gi, kb = iters[i]
                if kb == 0 and gi + 1 < len(groups):
                    load_q(gi + 1)
                Qg = Qgs[gi % 3]
                ps = psSs[cnt["ps"] % 3]
                cnt["ps"] += 1
                P = Pts[i % 4]
                ks = slice(kb * 128, (kb + 1) * 128)
                fw.mm(ps.t[:, 0, :], KT.t[0:64, ks], Qg.t[0:64, :], True, True, [KT, Qg], [ps], False)
                fw.mm(ps.t[:, 1, :], KT.t[64:128, ks], Qg.t[64:128, :], True, True, [KT, Qg], [ps], True)
                fw.op("act", lambda e: e.activation(out=P.t[:], in_=ps.t[:], func=AF.Exp, scale=0.125), [ps], [P])

            def emit_pv(i):
                gi, kb = iters[i]
                P = Pts[i % 4]
                for g in range(2):
                    fw.mm(psO[g].t[0:65, :], VB.t[:, kb, g * 65:(g + 1) * 65], P.t[:, g, :], kb == 0, kb == nkb - 1,
                          [VB, P], [psO[g]], g == 1)

            def fin1(gi):
                Osb = Osbs[gi % 2]
                for g in range(2):
                    fw.op("dve", lambda e: e.tensor_copy(out=Osb.t[0:65, g, :], in_=psO[g].t[0:65, :]), [psO[g]], [Osb])

            def fin2(gi):
                r, qg = groups[gi]
                q0 = qg * 512
                Osb = Osbs[gi % 2]
                psTt = psSs[cnt["ps"] % 3]
                cnt["ps"] += 1
                for g in range(2):
                    psTv = psTt.t[:, g, 0:260].rearrange("p (q d) -> p q d", d=65)
                    for qt in range(4):
                        fw.tr(psTv[:, qt, :], Osb.t[0:65, g, qt * 128:(qt + 1) * 128], identf.t[0:65, 0:65],
                              [Osb, identf], [psTt], qt == 3)
                    u = cnt["u"]
                    cnt["u"] += 1
                    rden = rdens[u % 2]
                    ob = obs[u % 3]
                    fw.op("dve", lambda e: e.reciprocal(out=rden.t[:], in_=psTv[:, :, 64:65]), [psTt], [rden])
                    rd_b = bass.AP(rden.t[:].tensor, rden.t[:].offset, [list(rden.t[:].ap[0]), [1, 4], [0, 64]])
                    fw.op("dve", lambda e: e.tensor_tensor(out=ob.t[:], in0=psTv[:, :, 0:64], in1=rd_b, op=ALU.mult),
                          [psTt, rden], [ob])
                    col0 = 256 + (g * 4 + r) * 64
                    dst = bass.AP(sq.OMIX.tensor, sq.OMIX.offset + q0 * D + col0, [[D, 128], [128 * D, 4], [1, 64]])
                    fw.store(dst, sq.OMIX_b, ob)

            load_q(0)
            deferred = []
            n = len(iters)
            for step in range(n + 1):
                if step < n:
                    emit_qk(step)
                if step >= 1:
                    i = step - 1
                    emit_pv(i)
                    gi, kb = iters[i]
                    if kb == nkb - 1:
                        fin1(gi)
                        deferred.append((step + 3, gi))
                while deferred and deferred[0][0] <= step:
                    fin2(deferred.pop(0)[1])
                if step % 12 == 5:
                    bg_step()
            for _, gi in deferred:
                fin2(gi)
            while bgi[0] < len(bg):
                bg_step()
        fw.barrier()

    def phase_C(self, sq, l):
        fw = self.fw
        S = sq.S
        J = S // 128
        ncols = 128 // J
        ngroups = 256 // ncols
        with ExitStack() as st:
            self.uid += 1
            X2t = st.enter_context(self.nc.sbuf_tensor("X2_%d" % self.uid, [128, 512, J], BF16))
            X2 = [Tile(X2t, Buf("X2_%d" % i)) for i in range(8)]
            Xcs = self.ring(st, 2, [128, J, 64], BF16, "Xc")
            xv = sq.AB.rearrange("(p j) c -> p j c", j=J)
            for ci in range(8):
                Xc = Xcs[ci % 2]
                fw.load(Xc, xv[:, :, ci * 64:(ci + 1) * 64], sq.AB_b)
                en = ("dve", "act", "pool")[ci % 3]
                src = Xc.t[:].rearrange("p j c -> p c j")
                dstv = X2t[:, ci * 64:(ci + 1) * 64, :]
                if en == "act":
                    fw.op("act", lambda e: e.copy(out=dstv, in_=src), [Xc], [X2[ci]])
                else:
                    fw.op(en, lambda e: e.tensor_copy(out=dstv, in_=src), [Xc], [X2[ci]])
            OC = self.sb(st, [128, J, 256], BF16, "OC")
            rhsA = self.const_tile(st, self.c_rhsA, [128, 256], BF16)
            rhsB = self.const_tile(st, self.c_rhsB, [128, 256], BF16)
            tw = self.const_tile(st, sq.tw, [128, 256], F32)
            bdcs = self.const_tile(st, sq.bdcs, [128, 256], BF16)
            ps1 = self.ring(st, 3, [128, 512], F32, "ps1", psum=True)
            ps2 = self.ring(st, 2, [128, 512], F32, "ps2", psum=True)
            t1s = self.ring(st, 3, [128, 2, 128], F32, "t1")
            t2s = self.ring(st, 3, [128, 2, 128], F32, "t2")
            Zs = self.ring(st, 4, [128, 2, 128], BF16, "Z")
            OCb = [Buf("OC%d" % i) for i in range(ngroups // 4)]

            def T1(gi):
                c0 = gi * ncols
                lhsA = X2t[:, c0:c0 + ncols, :].rearrange("p c j -> p (c j)")
                lhsB = X2t[:, 256 + c0:256 + c0 + ncols, :].rearrange("p c j -> p (c j)")
                XA_, XB_ = X2[c0 // 64], X2[(256 + c0) // 64]
                p1 = ps1[gi % 3]
                fw.mm(p1.t[:, 0:256], lhsA, rhsA.t[:], True, False, [XA_, rhsA], [p1], False)
                fw.mm(p1.t[:, 0:256], lhsB, rhsB.t[:], False, True, [XB_, rhsB], [p1], True)

            def T2(gi):
                p1 = ps1[gi % 3]
                t1, t2, Z = t1s[gi % 3], t2s[gi % 3], Zs[gi % 4]
                pv = p1.t[:, 0:256].rearrange("p (a f) -> p a f", a=2)
                fw.op("dve", lambda e: e.tensor_tensor(out=t1.t[:], in0=pv, in1=bc_mid(tw.t[:, 0:128], 2), op=ALU.mult),
                      [p1, tw], [t1])
                fw.op("dve", lambda e: e.tensor_tensor(out=t2.t[:], in0=pv, in1=bc_mid(tw.t[:, 128:256], 2), op=ALU.mult),
                      [p1, tw], [t2])
                fw.op("pool", lambda e: e.tensor_tensor(out=Z.t[:, 0, :], in0=t1.t[:, 0, :], in1=t2.t[:, 1, :], op=ALU.add),
                      [t1, t2], [Z])
                fw.op("pool", lambda e: e.tensor_tensor(out=Z.t[:, 1, :], in0=t1.t[:, 1, :], in1=t2.t[:, 0, :],
                                                        op=ALU.subtract), [t1, t2], [Z])

            def T3(gi):
                gq, gg = gi // 4, gi % 4
                p2 = ps2[gq % 2]
                Z = Zs[gi % 4]
                fw.mm(p2.t[:, gg * 128:(gg + 1) * 128], Z.t[:, 0, :], bdcs.t[:, 0:128], True, False, [Z, bdcs], [p2], False)
                fw.mm(p2.t[:, gg * 128:(gg + 1) * 128], Z.t[:, 1, :], bdcs.t[:, 128:256], False, True, [Z, bdcs], [p2], True)
                if gg == 3:
                    cb = gq * 4 * ncols
                    out_ap = OC.t[:, :, cb:cb + 4 * ncols].rearrange("p f c -> p c f")
                    in_ap = p2.t[:].rearrange("p (c f) -> p c f", f=J)
                    fw.op("act", lambda e: e.copy(out=out_ap, in_=in_ap), [p2], [OCb[gq]])

            for step in range(ngroups + 2):
                if step < ngroups:
                    T1(step)
                if 1 <= step <= ngroups:
                    T2(step - 1)
                if step >= 2:
                    T3(step - 2)
            OC.ssem = fw.dma_sem("s")
            dst = bass.AP(sq.OMIX.tensor, sq.OMIX.offset + 768, [[D, 128], [128 * D, J], [1, 256]])
            fw.dma("pool", dst, OC.t[:], OCb, [sq.OMIX_b], OC.ssem)
        fw.barrier()

    def phase_D(self, sq, l, x_src, x_src_b, x_dst, x_dst_b):
        fw = self.fw
        S = sq.S
        nt = S // 128
        with ExitStack() as st:
            Ws = [self.tail_ws(st) for _ in range(2)]
            for W in Ws:
                W["xns"] = self.ring(st, 2, [128, D], F32, "xn")
                W["t32"] = self.sb(st, [128, D], F32, "t32")
            eps = Ws[0]["eps"]
            identb = Ws[0]["identb"]
            wo = self.sb(st, [128, 8, D], BF16, "wo")
            fw.load(wo, self.W_OUT[l].rearrange("(k p) n -> p k n", p=128), self.W_OUT_b)
            gh8 = self.sb(st, [128, 8], F32, "gh8")
            with self.nc.allow_non_contiguous_dma(reason="tiny gain vector relayout"):
                fw.load(gh8, self.g_heads[l].rearrange("(k p) -> p k", p=128), self.in_buf)
            for k in range(8):
                fw.op("dve", lambda e: e.tensor_scalar(out=wo.t[:, k, :], in0=wo.t[:, k, :], scalar1=gh8.t[:, k:k + 1],
                                                       scalar2=None, op0=ALU.mult), [wo, gh8], [wo])
            gpost = self.gain_tile(st, self.g_mix_post[l:l + 1, :])
            gnext = self.gain_tile(st, self.g_ffn_pre[l:l + 1, :])
            oms = self.ring(st, 6, [128, D], BF16, "om")
            accs = [self.ring(st, 6, [128, 4, 65], F32, "acc%d" % i) for i in range(3)]
            xts = self.ring(st, 8, [128, D], F32, "xt")
            o32a = self.ring(st, 2, [128, 4, 64], F32, "o32a")
            junk = self.ring(st, 2, [128, 16, 64], F32, "junk")
            ss16 = self.ring(st, 2, [128, 16], F32, "ss16")
            rs16 = self.ring(st, 2, [128, 16], F32, "rs16")
            rden = self.ring(st, 2, [128, 4, 1], F32, "rdenD")
            obs = self.ring(st, 4, [128, D], BF16, "obD")
            oT = self.ring(st, 2, [128, 8, 128], BF16, "oT")
            psT2 = self.ring(st, 2, [128, 8, 128], BF16, "psT2", psum=True)
            psY = self.ring(st, 2, [128, D], F32, "psY", psum=True)

            def loads(ti):
                rs_ = slice(ti * 128, ti * 128 + 128)
                fw.load(oms[ti % 6], sq.OMIX[rs_, 256:1024], sq.OMIX_b, out=oms[ti % 6].t[:, 256:1024])
                for i in range(3):
                    fw.load(accs[i][ti % 6], sq.ACC[i][rs_, :].rearrange("p (h d) -> p h d", d=65), sq.ACC_b[i])
                fw.load(xts[ti % 8], x_src[rs_, :], x_src_b)

            def stageA(ti):
                c = ti % 2
                om = oms[ti % 6]
                a = [accs[i][ti % 6] for i in range(3)]
                ob = obs[ti % 4]
                oa, jk, s16, r16, rd = o32a[c], junk[c], ss16[c], rs16[c], rden[c]
                fw.op("pool", lambda e: e.tensor_tensor(out=a[0].t[:], in0=a[0].t[:], in1=a[1].t[:], op=ALU.add), [a[0], a[1]], [a[0]])
                fw.op("pool", lambda e: e.tensor_tensor(out=a[0].t[:], in0=a[0].t[:], in1=a[2].t[:], op=ALU.add), [a[0], a[2]], [a[0]])
                fw.op("dve", lambda e: e.reciprocal(out=rd.t[:], in_=a[0].t[:, :, 64:65]), [a[0]], [rd])
                rd_b = bass.AP(rd.t[:].tensor, rd.t[:].offset, [list(rd.t[:].ap[0]), [1, 4], [0, 64]])
                fw.op("dve", lambda e: e.tensor_tensor(out=oa.t[:], in0=a[0].t[:, :, 0:64], in1=rd_b, op=ALU.mult),
                      [a[0], rd], [oa])
                fw.op("act", lambda e: e.activation(out=jk.t[:, 0:4, :], in_=oa.t[:], func=AF.Square), [oa], [jk])
                fw.op("act", lambda e: e.activation(out=jk.t[:, 4:16, :],
                                                    in_=om.t[:, 256:1024].rearrange("p (h d) -> p h d", d=64),
                                                    func=AF.Square), [om], [jk])
                fw.op("dve", lambda e: e.tensor_reduce(out=s16.t[:], in_=jk.t[:], axis=AX.X, op=ALU.add), [jk], [s16])
                self.rstd(s16, r16, eps, 1.0 / 64.0)
                fw.op("dve", lambda e: e.tensor_tensor(out=ob.t[:, 0:256].rearrange("p (h d) -> p h d", d=64), in0=oa.t[:],
                                                       in1=bc_last(r16.t[:, 0:4], 64), op=ALU.mult), [oa, r16], [ob])
                fw.op("dve", lambda e: e.tensor_tensor(out=ob.t[:, 256:1024].rearrange("p (h d) -> p h d", d=64),
                                                       in0=om.t[:, 256:1024].rearrange("p (h d) -> p h d", d=64),
                                                       in1=bc_last(r16.t[:, 4:16], 64), op=ALU.mult), [om, r16], [ob])

            def stageB(ti):
                ob = obs[ti % 4]
                pT = psT2[ti % 2]
                for k in range(8):
                    fw.tr(pT.t[:, k, :], ob.t[:, k * 128:(k + 1) * 128], identb.t[:], [ob, identb], [pT], k == 7)
                o_T = oT[ti % 2]
                fw.op("act", lambda e: e.copy(out=o_T.t[:], in_=pT.t[:]), [pT], [o_T])
                pY = psY[ti % 2]
                for half in range(2):
                    hs = slice(half * 512, (half + 1) * 512)
                    for k in range(8):
                        fw.mm(pY.t[:, hs], o_T.t[:, k, :], wo.t[:, k, hs], k == 0, k == 7, [o_T, wo], [pY],
                              k == 7 and half == 1)

            def do_tail(ti):
                t0 = ti * 128
                pY = psY[ti % 2]
                return self.tail(pY.t[:], pY, xts[ti % 8], gpost, gnext, Ws[ti % 2], x_dst[t0:t0 + 128, :], x_dst_b,
                                 sq.H2T[:, :, 1 + t0:1 + t0 + 128], sq.H2T_b)

            for t in range(min(4, nt)):
                loads(t)
            l0, _ = fw.capture(lambda: stageA(0))
            l1, _ = fw.capture(lambda: stageA(1))
            fw.interleave(l0, l1)
            pends = []
            for p in range(nt // 2 + 1):
                for t in (2 * p + 4, 2 * p + 5):
                    if t < nt:
                        loads(t)
                lists = []
                newp = []
                for t in (2 * p + 2, 2 * p + 3):
                    if t < nt:
                        lst, _ = fw.capture(lambda t=t: stageA(t))
                        lists.append(lst)
                for t in (2 * p - 2, 2 * p - 1):
                    if 0 <= t < nt:
                        lst, pd = fw.capture(lambda t=t: do_tail(t))
                        lists.append(lst)
                        if pd:
                            newp.append(pd)
                if lists:
                    fw.interleave(*lists)
                for t in (2 * p, 2 * p + 1):
                    if t < nt:
                        stageB(t)
                for pd in pends:
                    pd()
                pends = newp
            for pd in pends:
                pd()
        fw.barrier()

    def phase_E(self, sq, l, x_src, x_src_b, x_dst, x_dst_b, last):
        fw = self.fw
        S = sq.S
        G = 512
        with ExitStack() as st:
            W = self.tail_ws(st)
            W["xns"] = self.ring(st, 2, [128, D], F32, "xnE")
            ysbs = self.ring(st, 2, [128, D], F32, "ysb")
            wd = self.sb(st, [128, 32, D], BF16, "wd")
            wd.b.multi = True
            wd.lsem = fw.dma_sem()
            wdv = self.WD[l].rearrange("(f p) n -> p f n", p=128)
            for i in range(4):
                fw.dma("sp", wd.t[:, i * 8:(i + 1) * 8, :], wdv[:, i * 8:(i + 1) * 8, :], [self.WD_b], [wd], wd.lsem)
            cwb = self.const_tile(st, self.cwb[l], [128, 4, 32], F32)
            gpost = self.gain_tile(st, self.g_ffn_post[l:l + 1, :])
            gnext = None if last else self.gain_tile(st, self.g_mix_pre[l + 1:l + 2, :])
            h2s = self.ring(st, 2, [128, 8, G + 2], BF16, "h2")
            wgs = self.ring(st, 3, [128, 8, 128], BF16, "wg")
            wus = self.ring(st, 3, [128, 8, 128], BF16, "wu")
            self.uid += 1
            A_t = st.enter_context(self.nc.sbuf_tensor("A_sb_%d" % self.uid, [128, 32, G], BF16))
            A = [Tile(A_t, Buf("A%d" % f)) for f in range(32)]
            gbufs = self.ring(st, 2, [128, G + 2], F32, "gbuf")
            cs = self.ring(st, 2, [128, G], F32, "cc")
            ges = self.ring(st, 2, [128, G], BF16, "ge")
            xts = self.ring(st, 2, [128, D], F32, "xtE")
            psG = self.ring(st, 2, [128, 1024], F32, "psG", psum=True)
            psU = self.ring(st, 2, [128, 512], F32, "psU", psum=True)
            psY = self.ps(st, [128, 512], F32, "psYE")
            it = 0
            pendE = [None]
            for gi in range(S // G):
                t0 = gi * G
                h2 = h2s[gi % 2]
                fw.load(h2, sq.H2T[:, :, t0:t0 + G + 2].rearrange("k p t -> p k t"), sq.H2T_b)
                for f in range(32):
                    wg, wu = wgs[it % 3], wus[it % 3]
                    pg, pu = psG[it % 2], psU[it % 2]
                    gbuf, c, ge = gbufs[it % 2], cs[it % 2], ges[it % 2]
                    it += 1
                    fw.load(wg, self.WG[l, f], self.WG_b)
                    fw.load(wu, self.WU[l, f], self.WU_b)
                    for k in range(8):
                        fw.mm(pg.t[:, 0:G], wg.t[:, k, :], h2.t[:, k, 1:G + 1], k == 0, k == 7, [wg, h2], [pg], False)
                        halo = bass.AP(h2.t[:].tensor, h2.t[:, k, 0:1].offset, [list(h2.t[:].ap[0]), [G + 1, 2]])
                        fw.mm(pg.t[:, G:G + 2], wg.t[:, k, :], halo, k == 0, k == 7, [wg, h2], [pg], k == 7)
                    for k in range(8):
                        fw.mm(pu.t[:], wu.t[:, k, :], h2.t[:, k, 1:G + 1], k == 0, k == 7, [wu, h2], [pu], k == 7)
                    fw.op("act", lambda e: e.copy(out=gbuf.t[:, 1:G + 1], in_=pg.t[:, 0:G]), [pg], [gbuf])
                    gh_out = bass.AP(gbuf.t[:].tensor, gbuf.t[:].offset, [list(gbuf.t[:].ap[0]), [G + 1, 2]])
                    fw.op("act", lambda e: e.copy(out=gh_out, in_=pg.t[:, G:G + 2]), [pg], [gbuf])
                    w0, w1, w2, bb = (cwb.t[:, j, f:f + 1] for j in range(4))
                    fw.op("dve", lambda e: e.tensor_scalar(out=c.t[:], in0=gbuf.t[:, 1:G + 1], scalar1=w1, scalar2=bb,
                                                           op0=ALU.mult, op1=ALU.add), [gbuf, cwb], [c])
                    fw.op("dve", lambda e: e.scalar_tensor_tensor(out=c.t[:], in0=gbuf.t[:, 0:G], scalar=w0, in1=c.t[:],
                                                                  op0=ALU.mult, op1=ALU.add), [gbuf, cwb, c], [c])
                    fw.op("dve", lambda e: e.scalar_tensor_tensor(out=c.t[:], in0=gbuf.t[:, 2:G + 2], scalar=w2, in1=c.t[:],
                                                                  op0=ALU.mult, op1=ALU.add), [gbuf, cwb, c], [c])
                    fw.op("act", lambda e: e.activation(out=ge.t[:], in_=c.t[:], func=AF.Gelu_apprx_tanh), [c], [ge])
                    fw.op("dve", lambda e: e.tensor_tensor(out=A_t[:, f, :], in0=ge.t[:], in1=pu.t[:], op=ALU.mult),
                          [ge, pu], [A[f]])
                    if f == 1 and pendE[0]:
                        pendE[0]()
                        pendE[0] = None
                for i in range(G // 128):
                    ts = slice(i * 128, (i + 1) * 128)
                    r0 = t0 + i * 128
                    xt = xts[i % 2]
                    ysb = ysbs[i % 2]
                    fw.load(xt, x_src[r0:r0 + 128, :], x_src_b)
                    for half in range(2):
                        hs = slice(half * 512, (half + 1) * 512)
                        for f in range(32):
                            fw.mm(psY.t[:], A_t[:, f, ts], wd.t[:, f, hs], f == 0, f == 31, [A[f], wd], [psY], f == 31)
                        fw.op("act", lambda e: e.copy(out=ysb.t[:, hs], in_=psY.t[:]), [psY], [ysb])
                    if pendE[0]:
                        pendE[0]()
                    pendE[0] = self.tail(ysb.t[:], ysb, xt, gpost, gnext, W, x_dst[r0:r0 + 128, :], x_dst_b,
                                         sq.HT_A[:, :, r0:r0 + 128], sq.HT_A_b, t32=ysb)
            if pendE[0]:
                pendE[0]()
        fw.barrier()

    def build(self):
        import os
        ph = os.environ.get("K_PHASES", "WI0ABCDEF")
        if "W" in ph:
            self.phase_W()
        for sq in self.seqs:
            if "I" in ph:
                self.phase_init(sq)
            if "0" in ph:
                self.phase_0(sq)
            for l in range(self.L):
                last = l == self.L - 1
                if "A" in ph:
                    self.phase_A(sq, l)
                if "B" in ph:
                    self.phase_B1(sq, l)
                if "C" in ph:
                    self.phase_B2(sq, l)
                if "D" in ph:
                    self.phase_C(sq, l)
                if "E" not in ph:
                    continue
                if l == 0:
                    xs, xsb = sq.x_in, self.in_buf
                else:
                    xs, xsb = sq.XB, sq.XB_b
                self.phase_D(sq, l, xs, xsb, sq.XA, sq.XA_b)
                if "F" not in ph:
                    continue
                if last:
                    self.phase_E(sq, l, sq.XA, sq.XA_b, sq.y_out, sq.y_b, True)
                else:
                    self.phase_E(sq, l, sq.XA, sq.XA_b, sq.XB, sq.XB_b, False)
        self.fw.barrier()


def _rope_tables(S):
    t = np.arange(S)
    inv = np.power(np.float32(500000.0), -np.arange(8, dtype=np.float32) / np.float32(8))
    ang = t.astype(np.float32)[:, None] * inv[None, :]
    c, s = np.cos(ang.astype(np.float64)), np.sin(ang.astype(np.float64))
    ropeA = np.concatenate([c, c, -s, s], axis=1).astype(np.float32)
    row = (t // 64).astype(np.float32)
    col = (t % 64).astype(np.float32)
    inv2 = np.power(np.float32(10000.0), -np.arange(16, dtype=np.float32) / np.float32(16))
    ar = (row[:, None] * inv2[None, :]).astype(np.float64)
    ac = (col[:, None] * inv2[None, :]).astype(np.float64)
    cr, sr, cc, sc = np.cos(ar), np.sin(ar), np.cos(ac), np.sin(ac)
    ropeB = np.concatenate([cr, cr, cc, cc, -sr, sr, -sc, sc], axis=1).astype(np.float32)
    return ropeA, ropeB


def _fft_consts(S):
    J = S // 128
    ncols = 128 // J
    m = np.arange(128)
    j = m % J
    cc = m // J
    fb = np.arange(128)
    ang = 2 * np.pi * (j[:, None] * fb[None, :]) / S
    tw = np.concatenate([np.cos(ang), np.sin(ang)], axis=1).astype(np.float32)
    fa = np.arange(J)
    n = np.arange(128)
    ncc, nfa = n // J, n % J
    same = (cc[:, None] == ncc[None, :])
    a2 = 2 * np.pi * (j[:, None] * nfa[None, :]) / J
    bdc = np.where(same, np.cos(a2), 0.0)
    bds = np.where(same, np.sin(a2), 0.0)
    bdcs = np.concatenate([bdc, bds], axis=1).astype(NPBF)
    return tw, bdcs


def _consts():
    c = {}
    c["c_identf"] = np.eye(128, dtype=np.float32)
    c["c_identb"] = np.eye(128, dtype=np.float32).astype(NPBF)
    k = np.arange(64)
    a = 2 * np.pi * (k[:, None] * k[None, :]) / 64
    bd = np.zeros((256, 512), np.float64)
    for g in range(4):
        bd[g * 64:(g + 1) * 64, g * 64:(g + 1) * 64] = np.cos(a)
        bd[g * 64:(g + 1) * 64, 256 + g * 64:256 + (g + 1) * 64] = np.sin(a)
    c["c_bd64"] = bd.astype(np.float32)
    i = np.arange(128)
    m = np.zeros((128, 2, 128), np.float32)
    m[:, 0, :] = (i[:, None] >= i[None, :])
    m[:, 1, :] = (i[:, None] <= i[None, :])
    c["c_masks"] = m.astype(NPBF)
    a = 2 * np.pi * (i[:, None] * i[None, :]) / 128
    C, Sn = np.cos(a), np.sin(a)
    c["c_rhsA"] = np.concatenate([C, -Sn], axis=1).astype(NPBF)
    c["c_rhsB"] = np.concatenate([-Sn, -C], axis=1).astype(NPBF)
    return c


def prep_weights(L, g_mix_pre, g_mix_post, w_in, g_q, g_k, g_heads, w_out, g_ffn_pre, g_ffn_post,
                 w_gate, w_up, conv_w, conv_b, w_down):
    f = lambda a: np.ascontiguousarray(np.asarray(a, dtype=np.float32))
    w_in = f(w_in)
    qa, ka, va = w_in[:, :, 0:256], w_in[:, :, 256:512], w_in[:, :, 512:768]
    qb, kb, vb, uc = w_in[:, :, 768:1280], w_in[:, :, 1280:1408], w_in[:, :, 1408:1536], w_in[:, :, 1536:1792]
    qbp = qb.reshape(L, D, 2, 4, 64).transpose(0, 1, 3, 2, 4).reshape(L, D, 512)
    m = {}
    m["w_in_r"] = np.ascontiguousarray(np.concatenate([qa, ka, qbp, kb, va, vb], axis=2))
    m["w_ucT"] = np.ascontiguousarray(uc.transpose(0, 2, 1))
    m["w_out"] = f(w_out)
    m["w_gate"] = f(w_gate)
    m["w_up"] = f(w_up)
    m["w_down"] = f(w_down)
    m["g_mix_pre"] = f(g_mix_pre)
    m["g_mix_post"] = f(g_mix_post)
    m["g_ffn_pre"] = f(g_ffn_pre)
    m["g_ffn_post"] = f(g_ffn_post)
    m["g_heads"] = f(g_heads)
    m["gqk"] = np.ascontiguousarray(np.stack([f(g_q), f(g_k)], axis=1))
    cw = f(conv_w).reshape(L, 3, 32, 128)
    cb = f(conv_b).reshape(L, 1, 32, 128)
    m["cwb"] = np.ascontiguousarray(np.concatenate([cw, cb], axis=1).transpose(0, 3, 1, 2))
    return m


_CACHE = {}


def get_builder(seq_lens, depth):
    key = (tuple(seq_lens), depth)
    if key not in _CACHE:
        _CACHE[key] = Builder(list(seq_lens), depth)
    return _CACHE[key]


def seq_consts(seq_lens):
    m = {}
    for S in set(seq_lens):
        ra, rb = _rope_tables(S)
        tw, bdcs = _fft_consts(S)
        m["ropeA_%d" % S] = ra
        m["ropeB_%d" % S] = rb
        m["tw_%d" % S] = tw
        m["bdcs_%d" % S] = bdcs
    return m


def kernel(x_prompt, x_sample, g_mix_pre, g_mix_post, w_in, g_q, g_k, g_heads, w_out,
           g_ffn_pre, g_ffn_post, w_gate, w_up, conv_w, conv_b, w_down):
    x_prompt = np.asarray(x_prompt, dtype=np.float32)
    x_sample = np.asarray(x_sample, dtype=np.float32)
    L = int(np.asarray(w_in).shape[0])
    SP, SS = x_prompt.shape[1], x_sample.shape[1]
    nB, nS = x_prompt.shape[0], x_sample.shape[0]
    b = get_builder((SP, SS), L)
    base = prep_weights(L, g_mix_pre, g_mix_post, w_in, g_q, g_k, g_heads, w_out, g_ffn_pre, g_ffn_post,
                        w_gate, w_up, conv_w, conv_b, w_down)
    base.update(_consts())
    base.update(seq_consts((SP, SS)))
    in_maps = []
    for c in range(8):
        m = dict(base)
        m["x0"] = np.ascontiguousarray(x_prompt[c % nB])
        m["x1"] = np.ascontiguousarray(x_sample[c % nS])
        in_maps.append(m)
    res = run_bass_kernel_spmd(b.nc, in_maps, core_ids=list(range(8)))
    yp = np.stack([np.asarray(res.results[c]["y0"], dtype=np.float32) for c in range(nB)], axis=0)
    ys = np.stack([np.asarray(res.results[c]["y1"], dtype=np.float32) for c in range(nS)], axis=0)
    return (yp, ys)
```
